# Optimizing a Trainium2 kernel written in Bass

```python
import math
import jax, jax.numpy as jnp
from jax import lax
import numpy as np

D_MODEL = 2048
BATCH = 4
SEQ = 4096
DEPTH = 1
DEC_BATCH = 8
DEC_SEQ = 16
PAST_LEN = 4096

CHUNK = 64
N_HEADS = 8
HEAD_DIM = 128
ATTN_WIDTH = N_HEADS * HEAD_DIM
POOL_WINDOWS = (2, 4, 8, 16)
POOL_GROUPS = len(POOL_WINDOWS)
POOL_WIDTH = D_MODEL // 2
POOL_GROUP_WIDTH = POOL_WIDTH // POOL_GROUPS
POOL_HIST = max(POOL_WINDOWS) - 1
MIX_WIDTH = POOL_WIDTH + ATTN_WIDTH
IN_WIDTH = POOL_WIDTH + 3 * ATTN_WIDTH + 2 * D_MODEL
D_FF = int(math.ceil(8 * D_MODEL / 3 / 256) * 256)
Q_BLOCK = 128
EPS = 1e-6

kernel_name = "pool_stickbreak_gated_streaming_encoder"


def rmsnorm(x, g):
    xf = x.astype(jnp.float32)
    xf = xf * lax.rsqrt(jnp.mean(xf * xf, axis=-1, keepdims=True) + EPS)
    return (xf * g.astype(jnp.float32)).astype(x.dtype)


def multiscale_pool(u_hist, u, pos0, w_pool, s_pool):
    B, T, P = u.shape
    up = jnp.concatenate([u_hist.astype(u.dtype), u], axis=1)
    upf = up.astype(jnp.float32)
    cs = jnp.concatenate([jnp.zeros((B, 1, P), jnp.float32), jnp.cumsum(upf, axis=1)], axis=1)
    end = cs[:, POOL_HIST + 1:POOL_HIST + 1 + T]
    pos = pos0 + jnp.arange(T, dtype=jnp.int32)
    outs = []
    for gi, w in enumerate(POOL_WINDOWS):
        sl = slice(gi * POOL_GROUP_WIDTH, (gi + 1) * POOL_GROUP_WIDTH)
        start = cs[:, POOL_HIST + 1 - w:POOL_HIST + 1 - w + T, sl]
        cnt = jnp.minimum(pos + 1, w).astype(jnp.float32)[None, :, None]
        outs.append((end[..., sl] - start) / cnt)
    pooled = jnp.concatenate(outs, axis=-1)
    diff = (pooled - upf[:, POOL_HIST:]).reshape(B, T, POOL_GROUPS, POOL_GROUP_WIDTH)
    o = jnp.einsum('btgc,gce->btge', diff, w_pool.astype(jnp.float32)).reshape(B, T, P)
    o = o * s_pool.astype(jnp.float32)
    return o.astype(u.dtype), up[:, -POOL_HIST:]


def stick_breaking(q, k, v, q_pos, k_pos):
    z = jnp.einsum('bqhd,bkhd->bhqk', q, k).astype(jnp.float32) * (HEAD_DIM ** -0.5)
    mask = k_pos[None, :] < q_pos[:, None]
    log_1m = jnp.where(mask, jax.nn.log_sigmoid(-z), 0.0)
    suffix = lax.cumsum(log_1m, axis=3, reverse=True) - log_1m
    w = jnp.where(mask, jnp.exp(jax.nn.log_sigmoid(z) + suffix), 0.0)
    return jnp.einsum('bhqk,bkhd->bqhd', w.astype(v.dtype), v)


def layer(x, pool_hist, k_past, v_past, g_mix, w_in, w_pool, s_pool, w_branch, w_out,
          g_ffn, w_gate_up, w_down):
    B, T, _ = x.shape
    pos0 = 0 if k_past is None else k_past.shape[1]
    n = rmsnorm(x, g_mix)
    proj = n @ w_in
    c0 = POOL_WIDTH
    c1 = c0 + ATTN_WIDTH
    c2 = c1 + ATTN_WIDTH
    c3 = c2 + ATTN_WIDTH
    u = proj[..., :c0]
    q = proj[..., c0:c1].reshape(B, T, N_HEADS, HEAD_DIM)
    k = proj[..., c1:c2].reshape(B, T, N_HEADS, HEAD_DIM)
    v = proj[..., c2:c3].reshape(B, T, N_HEADS, HEAD_DIM)
    gate_logits = proj[..., c3:]

    o_a, new_pool = multiscale_pool(pool_hist, u, pos0, w_pool, s_pool)

    if k_past is None:
        k_all, v_all = k, v
    else:
        k_all = jnp.concatenate([k_past.astype(k.dtype), k], axis=1)
        v_all = jnp.concatenate([v_past.astype(v.dtype), v], axis=1)
    k_pos = jnp.arange(k_all.shape[1], dtype=jnp.int32)
    q_pos = pos0 + jnp.arange(T, dtype=jnp.int32)
    if T > Q_BLOCK:
        nb = T // Q_BLOCK
        qb = q.reshape(B, nb, Q_BLOCK, N_HEADS, HEAD_DIM).transpose(1, 0, 2, 3, 4)
        pb = q_pos.reshape(nb, Q_BLOCK)
        ob = lax.map(lambda a: stick_breaking(a[0], k_all, v_all, a[1], k_pos), (qb, pb))
        o_b = ob.transpose(1, 0, 2, 3, 4).reshape(B, T, ATTN_WIDTH)
    else:
        o_b = stick_breaking(q, k_all, v_all, q_pos, k_pos).reshape(B, T, ATTN_WIDTH)

    y_a = o_a @ w_branch[:POOL_WIDTH]
    y_b = o_b @ w_branch[POOL_WIDTH:]
    g = jax.nn.sigmoid(gate_logits.astype(jnp.float32))
    merged = g[..., :D_MODEL] * y_a.astype(jnp.float32) + g[..., D_MODEL:] * y_b.astype(jnp.float32)
    h = x + merged.astype(x.dtype) @ w_out

    n2 = rmsnorm(h, g_ffn)
    gu = n2 @ w_gate_up
    hid = jax.nn.silu(gu[..., :D_FF]) * gu[..., D_FF:]
    h = h + hid @ w_down
    return h, k, v, new_pool


def setup_inputs(seed: int = 0) -> dict:
    key = jax.random.key(seed)
    ks = jax.random.split(key, 16)
    f32 = jnp.float32
    nrm = lambda k, s, sc: jax.random.normal(k, s, f32) * sc
    return {
        "x_prompt": nrm(ks[0], (BATCH, SEQ, D_MODEL), 1.0),
        "x_sample": nrm(ks[1], (DEC_BATCH, DEC_SEQ, D_MODEL), 1.0),
        "cache_k": nrm(ks[2], (DEPTH, DEC_BATCH, PAST_LEN, N_HEADS, HEAD_DIM), 1.0),
        "cache_v": nrm(ks[3], (DEPTH, DEC_BATCH, PAST_LEN, N_HEADS, HEAD_DIM), 1.0),
        "state_pool": nrm(ks[4], (DEPTH, DEC_BATCH, POOL_HIST, POOL_WIDTH), 1.0),
        "g_mix": 1.0 + nrm(ks[5], (DEPTH, D_MODEL), 0.02),
        "w_in": nrm(ks[6], (DEPTH, D_MODEL, IN_WIDTH), D_MODEL ** -0.5),
        "w_pool": nrm(ks[7], (DEPTH, POOL_GROUPS, POOL_GROUP_WIDTH, POOL_GROUP_WIDTH), POOL_GROUP_WIDTH ** -0.5),
        "s_pool": 1.0 + nrm(ks[8], (DEPTH, POOL_WIDTH), 0.02),
        "w_branch": nrm(ks[9], (DEPTH, MIX_WIDTH, D_MODEL), POOL_WIDTH ** -0.5),
        "w_out": nrm(ks[10], (DEPTH, D_MODEL, D_MODEL), D_MODEL ** -0.5),
        "g_ffn": 1.0 + nrm(ks[11], (DEPTH, D_MODEL), 0.02),
        "w_gate_up": nrm(ks[12], (DEPTH, D_MODEL, 2 * D_FF), D_MODEL ** -0.5),
        "w_down": nrm(ks[13], (DEPTH, D_FF, D_MODEL), D_FF ** -0.5),
        "g_final": 1.0 + nrm(ks[14], (D_MODEL,), 0.02),
    }


def reference(x_prompt, x_sample, cache_k, cache_v, state_pool, g_mix, w_in, w_pool, s_pool,
              w_branch, w_out, g_ffn, w_gate_up, w_down, g_final):
    hp = x_prompt
    hs = x_sample
    kp_l, vp_l, pp_l, ks_l, vs_l, ps_l = [], [], [], [], [], []
    for l in range(DEPTH):
        params = (g_mix[l], w_in[l], w_pool[l], s_pool[l], w_branch[l], w_out[l],
                  g_ffn[l], w_gate_up[l], w_down[l])
        zero_hist = jnp.zeros((hp.shape[0], POOL_HIST, POOL_WIDTH), hp.dtype)
        hp, kp, vp, pp = layer(hp, zero_hist, None, None, *params)
        hs, ksm, vsm, psm = layer(hs, state_pool[l], cache_k[l], cache_v[l], *params)
        kp_l.append(kp); vp_l.append(vp); pp_l.append(pp)
        ks_l.append(ksm); vs_l.append(vsm); ps_l.append(psm)
    y_prompt = rmsnorm(hp, g_final)
    y_sample = rmsnorm(hs, g_final)
    return (y_prompt, y_sample, jnp.stack(kp_l), jnp.stack(vp_l), jnp.stack(pp_l),
            jnp.stack(ks_l), jnp.stack(vs_l), jnp.stack(ps_l))
```

```python
import os
import numpy as np
import concourse.bass as bass
import concourse.mybir as mybir
from concourse.bass_utils import run_bass_kernel_spmd

F32 = mybir.dt.float32
BF16 = mybir.dt.bfloat16
AF = mybir.ActivationFunctionType
ALU = mybir.AluOpType

D = 2048
DC = 16
NH = 8
HD = 128
PW = 1024
DFF = 5632
FC = 44
INW = 8192
NS = 16
EPS = 1e-6
TN = 512
WIN = (2, 4, 8, 16)
NCORES = 8
WRING = 3
EPOCH = 24000


class _Stop(Exception):
    pass


class Res:
    __slots__ = ("last_w", "readers", "name")

    def __init__(self, name=""):
        self.last_w = None
        self.readers = []
        self.name = name


class Op:
    __slots__ = ("eng", "fn", "deps", "signaled", "sig", "dma", "idx")

    def __init__(self, eng, fn, dma):
        self.eng = eng
        self.fn = fn
        self.deps = ()
        self.signaled = False
        self.sig = None
        self.dma = dma


class Prog:
    ENGS = ("pe", "act", "dve", "pool", "sp")

    def __init__(self, dry=False):
        self.dry = dry
        self.streams = {e: [] for e in self.ENGS}

    def op(self, eng, fn, reads=(), writes=(), dma=False):
        if self.dry:
            return None
        o = Op(eng, fn, dma)
        deps = {}
        for r in reads:
            w = r.last_w
            if w is not None:
                deps[id(w)] = w
        for wr in writes:
            w = wr.last_w
            if w is not None:
                deps[id(w)] = w
            for rd in wr.readers:
                deps[id(rd)] = rd
        if eng == "pe" and not dma:
            deps = {k: v for k, v in deps.items() if v.dma or v.eng != "pe"}
        o.deps = tuple(deps.values())
        for d in o.deps:
            d.signaled = True
        for r in reads:
            r.readers.append(o)
        for wr in writes:
            wr.last_w = o
            wr.readers = []
        self.streams[eng].append(o)
        return o


def alias(new, old):
    pend = []
    for r in old:
        if r.last_w is not None:
            pend.append(r.last_w)
        pend.extend(r.readers)
    if not pend:
        return
    for r in new:
        r.readers = r.readers + pend


def build(n_own_tiles, n_prior_tiles, past_blks):
    NOWN = n_own_tiles * TN
    NPRI = n_prior_tiles * TN
    SEQL = NOWN + NPRI
    NBLK = SEQL // 128
    PAST = past_blks * 128
    MAXKB = max(NBLK, past_blks + 1)

    nc = bass.Bass("TRN2", target_bir_lowering=False)

    def din(name, shape, dt=F32):
        return nc.dram_tensor(name, list(shape), dt, kind="ExternalInput").ap()

    def dout(name, shape, dt=F32):
        return nc.dram_tensor(name, list(shape), dt, kind="ExternalOutput").ap()

    xp_own = din("xp_own", [NOWN, D])
    xp_pri = din("xp_pri", [NPRI, D])
    xs_in = din("xs_in", [NS, D])
    ck_in = din("ck_in", [PAST, NH * HD])
    cv_in = din("cv_in", [PAST, NH * HD])
    sp_in = din("sp_in", [15, PW])
    w_in = din("w_in", [D, INW])
    w_pool = din("w_pool", [4, 256, 256])
    w_branch = din("w_branch", [D, D])
    w_out = din("w_out", [D, D])
    w_gu = din("w_gu", [D, 2 * DFF])
    w_down = din("w_down", [DFF, D])
    gmix_in = din("gmix", [128, DC])
    gffn_in = din("gffn", [128, DC])
    spool_in = din("spool", [128, 8])
    gfin_in = din("gfin", [128, D])
    cnt_in = din("cntinv", [128, 4 * 16])
    ident_in = din("ident", [128, 128])
    negtri_in = din("negtri", [128, 128])
    negones_in = din("negones", [128, 128])
    masks_in = din("masks", [128, 4 * TN])

    y_own = dout("y_own", [NOWN, D])
    nk_own = dout("nk_own", [NOWN, NH * HD])
    nv_own = dout("nv_own", [NOWN, NH * HD])
    pool_p = dout("pool_p", [15, PW])
    y_s = dout("y_s", [NS, D])
    nk_s = dout("nk_s", [NS, NH * HD])
    nv_s = dout("nv_s", [NS, NH * HD])
    pool_s = dout("pool_s", [15, PW])

    kt_scr = nc.dram_tensor("kt_scr", [128, NH, SEQL], BF16, kind="Internal").ap()
    v_scr = nc.dram_tensor("v_scr", [128, NH, NBLK, HD], BF16, kind="Internal").ap()
    kts_scr = nc.dram_tensor("kts_scr", [128, NH, NS], BF16, kind="Internal").ap()
    vs_scr = nc.dram_tensor("vs_scr", [NS, NH * HD], BF16, kind="Internal").ap()

    XBYTES = 99072
    from contextlib import ExitStack
    es = ExitStack()

    def sb(name, shape, dt):
        return es.enter_context(nc.sbuf_tensor(name, list(shape), dt))

    X = sb("X", [128, XBYTES // 4], F32)
    wring = [sb(f"wr{i}", [128, 16, 512], BF16) for i in range(WRING)]
    xnT = sb("xnT", [128, DC, TN], BF16)
    QT = sb("QT", [128, NH, TN], BF16)
    oaT = sb("oaT", [128, 8, TN], BF16)
    obT = sb("obT", [128, NH, TN], BF16)
    ident = sb("identb", [128, 128], BF16)
    negtri = sb("negtrib", [128, 128], BF16)
    negones = sb("negonesb", [128, 128], BF16)
    masks = sb("masksb", [128, 4, TN], BF16)
    identf = sb("identf", [128, 128], F32)
    gmix = sb("gmixs", [128, DC], F32)
    gffn = sb("gffns", [128, DC], F32)
    spool = sb("spools", [128, 8], F32)
    gfin = sb("gfins", [128, D], F32)
    cntinv = sb("cntinvs", [128, 4, 16], F32)
    wpool = sb("wpools", [128, 4, 2, 256], BF16)
    stats = sb("stats", [128, 16], F32)
    utok = sb("utok", [NS, PW], F32)
    uhist = sb("uhist", [128, 8, 16], F32)

    def xv(off, nbytes, dt, pat=None, **kw):
        a = X[:, off // 4:(off + nbytes) // 4]
        if dt == BF16:
            a = a.bitcast(BF16)
        if pat:
            a = a.rearrange(pat, **kw)
        return a

    xs_b = [xv(0, 8192, F32), xv(8192, 8192, F32)]
    xnt = xv(16384, 16384, BF16, "p (t f) -> p t f", t=4)
    UW = 16 + TN
    uT = xv(32768, 8 * UW * 4, F32, "p (c n) -> p c n", c=8)
    o = 32768 + 8 * UW * 4
    tA = xv(o, 2 * UW * 4, F32, "p (c n) -> p c n", c=2)
    o += 2 * UW * 4
    tB = xv(o, 2 * UW * 4, F32, "p (c n) -> p c n", c=2)
    o += 2 * UW * 4
    diffT = xv(o, 8192, BF16, "p (c n) -> p c n", c=8)
    o += 8192
    kvs = [xv(o + i * 2048, 2048, F32) for i in range(4)]
    o += 8192
    kb = xv(o, 8192, BF16, "p (t f) -> p t f", t=4)
    o += 8192
    vb = xv(o, 8192, BF16, "p (h t d) -> p h t d", h=NH, t=4)
    o += 8192
    KTt = xv(o, 8192, BF16, "p (c n) -> p c n", c=8)
    o += 8192
    assert o <= XBYTES, o
    xnt_alt = xv(32768, 16384, BF16, "p (t f) -> p t f", t=4)
    xnT_alt = xv(49152, 16384, BF16, "p (c n) -> p c n", c=DC)
    KB2 = MAXKB * 128 * 2
    assert KB2 <= 9216
    KTh = [xv(0, 9216, BF16), xv(9216, 9216, BF16)]
    Vh = [xv(18432, 9216, BF16, "p (j d) -> p j d", d=HD), xv(27648, 9216, BF16, "p (j d) -> p j d", d=HD)]
    Eb = [xv(36864 + 2048 * i, 1024, BF16) for i in range(3)]
    Lb = [xv(43008 + 1024 * i, 1024, BF16) for i in range(3)]
    Ab = [xv(46080 + 1024 * i, 1024, BF16) for i in range(3)]
    S32 = xv(49152, 2048, F32)
    Sbb = [xv(51200 + 1024 * i, 1024, BF16) for i in range(3)]
    ckst = xv(54272, 8192, BF16, "p (j d) -> p j d", d=HD)
    ht = xv(0, 32768, F32, "p (t f) -> p t f", t=4)
    hidT = xv(32768, 45056, BF16, "p (c n) -> p c n", c=FC)
    mergedT = xv(32768, 16384, BF16, "p (c n) -> p c n", c=DC)
    n2t = xv(32768, 16384, BF16, "p (t f) -> p t f", t=4)
    t1b = [xv(77824 + 2048 * i, 2048, F32) for i in range(4)]
    sAb = [xv(86016, 2048, F32)]
    sBb = [xv(88064, 2048, F32)]

    banks = [es.enter_context(nc.psum_tensor(f"bank{i}", [128, 512], F32)) for i in range(8)]

    NDL = 20
    sems_eng = {e: [es.enter_context(nc.semaphore(f"s_{e}{k}")) for k in range(4)] for e in ("pe", "act", "dve", "pool")}
    sems_dma = [es.enter_context(nc.semaphore(f"s_dma{k}")) for k in range(NDL)]

    R = {}

    def res(name):
        if name not in R:
            R[name] = Res(name)
        return R[name]

    r_xs = [res("xs0"), res("xs1")]
    r_xnt = [res(f"xnt{t}") for t in range(4)]
    r_xnT = [res(f"xnT{k}") for k in range(DC)]
    r_uT = res("uT")
    r_xnt_alt = [res(f"xntalt{t}") for t in range(4)]
    r_xnT_alt = [res(f"xnTalt{k}") for k in range(DC)]
    r_tA, r_tB = res("tA"), res("tB")
    r_diff = [res(f"diff{g}") for g in range(4)]
    r_kvs = [res(f"kvs{i}") for i in range(4)]
    r_kb, r_vb, r_KTt = res("kb"), res("vb"), res("KTt")
    r_QT = [res(f"QT{h}") for h in range(NH)]
    r_oaT = [res(f"oaT{c}") for c in range(8)]
    r_obT = [res(f"obT{h}") for h in range(NH)]
    r_KTh = [res("KTh0"), res("KTh1")]
    r_Vh = [res("Vh0"), res("Vh1")]
    r_E = [res(f"E{i}") for i in range(3)]
    r_L = [res(f"L{i}") for i in range(3)]
    r_A = [res(f"A{i}") for i in range(3)]
    r_S32 = res("S32")
    r_Sb = [res(f"Sb{i}") for i in range(3)]
    r_ckst = res("ckst")
    r_ht = [res(f"ht{t}") for t in range(4)]
    r_hid = [res(f"hid{c}") for c in range(FC)]
    r_mer = [res(f"mer{c}") for c in range(DC)]
    r_n2t = [res(f"n2t{t}") for t in range(4)]
    r_sA = [res("sA0")]
    r_sB = [res("sB0")]
    r_t1 = [res(f"t1{i}") for i in range(4)]
    r_bank = [res(f"bank{i}") for i in range(8)]
    r_wr = [res(f"wr{i}") for i in range(WRING)]
    r_const = res("const")
    r_stats = [res(f"st{i}") for i in range(16)]
    r_utok = res("utok")
    r_uhist = res("uhist")
    r_ktscr = [res(f"ktscr{t}") for t in range(SEQL // TN)]
    r_vscr = [res(f"vscr{t}") for t in range(SEQL // TN)]
    r_sscr = res("sscr")
    r_out = res("out")

    XA = r_xs + r_xnt + [r_uT, r_tA, r_tB] + r_diff + r_kvs + [r_kb, r_vb, r_KTt]
    XALT = r_xnt_alt + r_xnT_alt
    XC = r_KTh + r_Vh + r_E + r_L + r_A + [r_S32] + r_Sb + [r_ckst]
    XD = r_ht + r_hid + r_mer + r_n2t + r_sA + r_sB + r_t1

    def flow(prog, wlist):
        wreq = []
        wstate = {"next_load": 0, "released": 0}
        last_out = []

        def OP(eng, fn, reads=(), writes=(), dma=False):
            xb = [r for r in reads if r.name.startswith("bank")]
            if xb:
                writes = list(writes) + xb
            return prog.op(eng, fn, reads, writes, dma)

        ckstate = {"n": 0}
        kstop = int(os.environ.get("KSTOP", "100000"))

        def ck(label=""):
            ckstate["n"] += 1
            if ckstate["n"] == kstop:
                print("KSTOP at", ckstate["n"], label, flush=True)
                raise _Stop()

        wseen = set()

        def wload(i):
            key, src, kcn = wlist[i]
            slot = i % WRING
            pid = wpid[key]
            if key not in wseen:
                wseen.add(key)
                OP("pool", lambda e, s=slot, src=src, k=kcn: e.dma_start(out=wring[s][:, 0:k, :], in_=src),
                   reads=[], writes=[r_wr[slot]], dma=True)
                OP("sp", lambda e, s=slot, pid=pid, k=kcn: e.dma_start(out=wscr[pid, :, 0:k, :], in_=wring[s][:, 0:k, :]),
                   reads=[r_wr[slot]], writes=[r_wscr[pid]], dma=True)
            else:
                OP("sp", lambda e, s=slot, pid=pid, k=kcn: e.dma_start(out=wring[s][:, 0:k, :], in_=wscr[pid, :, 0:k, :]),
                   reads=[r_wscr[pid]], writes=[r_wr[slot]], dma=True)

        def wfill():
            if prog.dry:
                return
            while wstate["next_load"] < min(len(wlist), wstate["released"] + WRING):
                wload(wstate["next_load"])
                wstate["next_load"] += 1

        def wget(srckey, kcn):
            i = len(wreq)
            wreq.append((srckey[0], srckey[1], kcn))
            assert i - wstate["released"] < WRING
            wfill()
            return wring[i % WRING], r_wr[i % WRING]

        def wrel():
            wstate["released"] += 1
            wfill()

        def wsrc(w, r0, r1, c0, c1):
            return ((w.tensor.name, r0, r1, c0, c1), w[r0:r1, c0:c1].rearrange("(kc p) m -> p kc m", p=128))

        free = list(range(8))

        def acq():
            return free.pop(0)

        def rel(b):
            free.append(b)

        def cload(dst, src, eng="sp"):
            OP(eng, lambda e, d=dst, s=src: e.dma_start(out=d, in_=s), reads=[], writes=[r_const], dma=True)

        cload(ident[:], ident_in, "pool")
        cload(negtri[:], negtri_in, "pool")
        cload(negones[:], negones_in, "pool")
        cload(masks[:], masks_in.rearrange("p (j t) -> p j t", j=4), "pool")
        cload(wpool[:], w_pool.rearrange("g (cc p) e -> p g cc e", p=128), "pool")
        cload(identf[:], ident_in)
        cload(gmix[:], gmix_in)
        cload(gffn[:], gffn_in)
        cload(spool[:], spool_in)
        cload(gfin[:], gfin_in)
        cload(cntinv[:], cnt_in.rearrange("p (g t) -> p g t", g=4))
        OP("dve", lambda e: e.memset(uT[:], 0.0), reads=[], writes=[r_uT])
        def rmsnorm_rows(src_ap, r_src, pn, dst_ap, r_dst, si, gscale=None):
            rs = [r_stats[si * 3 + 0], r_stats[si * 3 + 1], r_stats[si * 3 + 2]]
            c0 = si * 3
            ss, rt, rstd = stats[:pn, c0:c0 + 1], stats[:pn, c0 + 1:c0 + 2], stats[:pn, c0 + 2:c0 + 3]
            junk = dst_ap if gscale is None else None
            OP("act", lambda e: e.activation(out=junk if junk is not None else n2t[:pn, si, :], in_=src_ap, func=AF.Square, accum_out=ss),
               reads=[r_src], writes=[rs[0], r_dst if gscale is None else r_n2t[si]])
            OP("act", lambda e: e.activation(out=rt, in_=ss, func=AF.Sqrt, scale=1.0 / D, bias=EPS),
               reads=[rs[0]], writes=[rs[1]])
            OP("dve", lambda e: e.reciprocal(rstd, rt), reads=[rs[1]], writes=[rs[2]])
            if gscale is None:
                OP("dve", lambda e: e.tensor_scalar(dst_ap, src_ap, rstd, None, ALU.mult),
                   reads=[r_src, rs[2]], writes=[r_dst])
            else:
                OP("dve", lambda e: e.scalar_tensor_tensor(dst_ap, src_ap, rstd, gscale, ALU.mult, ALU.mult),
                   reads=[r_src, rs[2], r_const], writes=[r_dst])

        def transposes_to_T(src_t, r_src_t, NB, pn, N, gvec, xnT=xnT, r_xnT=r_xnT):
            for kc2 in range(DC // 2):
                b = acq()
                bb = banks[b][:, :].bitcast(BF16).rearrange("p (c n) -> p c n", c=2)
                for kk in range(2):
                    kc = kc2 * 2 + kk
                    for tb in range(NB):
                        OP("pe", lambda e, kk=kk, kc=kc, tb=tb, bb=bb: e.transpose(
                            out=bb[:, kk, tb * 128:tb * 128 + pn], in_=src_t[:pn, tb, kc * 128:(kc + 1) * 128],
                            identity=ident[:pn, :pn]),
                           reads=[r_src_t[tb], r_const], writes=[r_bank[b]])
                for kk in range(2):
                    kc = kc2 * 2 + kk
                    eng = "dve"
                    if eng == "act":
                        OP("act", lambda e, kk=kk, kc=kc, bb=bb: e.activation(
                            out=xnT[:, kc, :N], in_=bb[:, kk, :N], func=AF.Copy, scale=gvec[:, kc:kc + 1]),
                           reads=[r_bank[b], r_const], writes=[r_xnT[kc]])
                    else:
                        OP("dve", lambda e, kk=kk, kc=kc, bb=bb: e.tensor_scalar(
                            xnT[:, kc, :N], bb[:, kk, :N], gvec[:, kc:kc + 1], None, ALU.mult),
                           reads=[r_bank[b], r_const], writes=[r_xnT[kc]])
                rel(b)

        def proj_fm(wsl, r_w, mcl, kcn, rhs_fn, r_rhs, N, kofs=0):
            b = acq()
            for kc in range(kcn):
                OP("pe", lambda e, kc=kc, b=b: e.matmul(
                    banks[b][:, :N], wsl[:, kofs + kc, mcl * 128:(mcl + 1) * 128], rhs_fn(kc),
                    start=(kc == 0), stop=(kc == kcn - 1)),
                   reads=[r_w, r_rhs[kc]], writes=[r_bank[b]])
            return b

        def tile(kind, g, x_rows, N, tok0):
            NB = (N + 127) // 128
            pn = min(N, 128)
            last_prior = (kind == "prior" and g == n_prior_tiles - 1)
            last_own = (kind == "own" and g == n_own_tiles - 1)
            need_utok = last_own or kind == "sample"
            full = kind != "prior"
            use_alt = (kind == "prior" and g % 2 == 1)
            xnt_l, r_xnt_l = (xnt_alt, r_xnt_alt) if use_alt else (xnt, r_xnt)
            xnT_l, r_xnT_l = (xnT_alt, r_xnT_alt) if use_alt else (xnT, r_xnT)

            if kind == "prior":
                if use_alt:
                    alias(XALT, [r_uT, r_tA, r_tB] + r_diff + r_kvs)
            else:
                alias(XA, XD + XC + XALT)
            for tb in range(NB):
                xsb, rxs = xs_b[tb % 2], r_xs[tb % 2]
                OP("sp", lambda e, tb=tb, xsb=xsb: e.dma_start(out=xsb[:pn, :], in_=x_rows[tb * 128:tb * 128 + pn, :]),
                   reads=[], writes=[rxs], dma=True)
                rmsnorm_rows(xsb[:pn, :], rxs, pn, xnt_l[:pn, tb, :], r_xnt_l[tb], tb)
            ck("A: norm done")
            transposes_to_T(xnt_l, r_xnt_l, NB, pn, N, gmix, xnT_l, r_xnT_l)
            ck("A: transposes done")

            if full or last_prior:
                for p in range(2):
                    wsl, rw = wget(wsrc(w_in, 0, D, 512 * p, 512 * p + 512), DC)
                    if full:
                        for mcl in range(4):
                            mc = 4 * p + mcl
                            b = proj_fm(wsl, rw, mcl, DC, lambda kc: xnT_l[:, kc, :N], r_xnT_l, N)
                            OP("act", lambda e, b=b, mc=mc: e.copy(uT[:, mc, 16:16 + N], banks[b][:, :N]),
                               reads=[r_bank[b]], writes=[r_uT])
                            rel(b)
                    else:
                        for mcl in range(4):
                            mc = 4 * p + mcl
                            b = proj_fm(wsl, rw, mcl, DC, lambda kc: xnT_l[:, kc, N - 16:N], r_xnT_l, 16)
                            OP("act", lambda e, b=b, mc=mc: e.copy(uhist[:, mc, :], banks[b][:, :16]),
                               reads=[r_bank[b]], writes=[r_uhist])
                            rel(b)
                    if need_utok:
                        b = acq()
                        for kc in range(DC):
                            OP("pe", lambda e, kc=kc, b=b, wsl=wsl: e.matmul(
                                banks[b][:NS, :512], xnT_l[:, kc, N - 16:N], wsl[:, kc, :512],
                                start=(kc == 0), stop=(kc == DC - 1)),
                               reads=[rw, r_xnT_l[kc]], writes=[r_bank[b]])
                        OP("act", lambda e, b=b, p=p: e.copy(utok[:NS, p * 512:(p + 1) * 512], banks[b][:NS, :512]),
                           reads=[r_bank[b]], writes=[r_utok])
                        rel(b)
                    wrel()
                if need_utok:
                    dst = pool_p if kind == "own" else pool_s
                    OP("sp", lambda e, dst=dst: e.dma_start(out=dst[:, :], in_=utok[1:NS, :]),
                       reads=[r_utok], writes=[r_out], dma=True)
                    last_out.append(1)

            ck("A: u proj done")
            if full:
                for p in range(2):
                    wsl, rw = wget(wsrc(w_in, 0, D, PW + 512 * p, PW + 512 * p + 512), DC)
                    for mcl in range(4):
                        hh = 4 * p + mcl
                        b = proj_fm(wsl, rw, mcl, DC, lambda kc: xnT_l[:, kc, :N], r_xnT_l, N)
                        OP("act", lambda e, b=b, hh=hh: e.activation(
                            out=QT[:, hh, :N], in_=banks[b][:, :N], func=AF.Copy, scale=float(HD ** -0.5)),
                           reads=[r_bank[b]], writes=[r_QT[hh]])
                        rel(b)
                    wrel()

            ck("A: q proj done")
            nko, nvo = (nk_own, nv_own) if kind == "own" else (nk_s, nv_s)
            ksi = 0
            for which in range(2):
                dstb, r_dstb = (kb, r_kb) if which == 0 else (vb, r_vb)
                oten = nko if which == 0 else nvo
                for p in range(2):
                    c0 = 2 * PW + which * PW + 512 * p
                    wsl, rw = wget(wsrc(w_in, 0, D, c0, c0 + 512), DC)
                    for tb in range(NB):
                        b = acq()
                        for kc in range(DC):
                            OP("pe", lambda e, kc=kc, b=b, tb=tb, wsl=wsl: e.matmul(
                                banks[b][:pn, :512], xnT_l[:, kc, tb * 128:tb * 128 + pn], wsl[:, kc, :512],
                                start=(kc == 0), stop=(kc == DC - 1)),
                               reads=[rw, r_xnT_l[kc]], writes=[r_bank[b]])
                        if full:
                            ki = ksi % 4
                            ksi += 1
                            OP("act", lambda e, b=b, ki=ki: e.copy(kvs[ki][:pn, :], banks[b][:pn, :512]),
                               reads=[r_bank[b]], writes=[r_kvs[ki]])
                            r0 = (g * TN if kind == "own" else 0) + tb * 128
                            OP("sp", lambda e, ki=ki, oten=oten, r0=r0, p=p: e.dma_start(
                                out=oten[r0:r0 + pn, 512 * p:512 * p + 512], in_=kvs[ki][:pn, :]),
                               reads=[r_kvs[ki]], writes=[r_out], dma=True)
                        if which == 0:
                            OP("dve", lambda e, b=b, tb=tb, p=p: e.tensor_copy(
                                kb[:pn, tb, 512 * p:512 * p + 512], banks[b][:pn, :512]),
                               reads=[r_bank[b]], writes=[r_dstb])
                        else:
                            OP("dve", lambda e, b=b, tb=tb, p=p: e.tensor_copy(
                                vb[:pn, 4 * p:4 * p + 4, tb, :], banks[b][:pn, :512].rearrange("p (h d) -> p h d", h=4)),
                               reads=[r_bank[b]], writes=[r_dstb])
                        rel(b)
                    wrel()
            ck("A: kv proj done")
            for h2 in range(NH // 2):
                b = acq()
                bb = banks[b][:, :].bitcast(BF16).rearrange("p (c n) -> p c n", c=2)
                for kk in range(2):
                    hh = 2 * h2 + kk
                    for tb in range(NB):
                        OP("pe", lambda e, kk=kk, hh=hh, tb=tb, bb=bb: e.transpose(
                            out=bb[:, kk, tb * 128:tb * 128 + pn], in_=kb[:pn, tb, hh * 128:(hh + 1) * 128],
                            identity=ident[:pn, :pn]),
                           reads=[r_kb, r_const], writes=[r_bank[b]])
                OP("dve", lambda e, h2=h2, bb=bb: e.tensor_copy(KTt[:, 2 * h2:2 * h2 + 2, :N], bb[:, :, :N]),
                   reads=[r_bank[b]], writes=[r_KTt])
                rel(b)
            ck("A: KT transposes done")
            if kind == "sample":
                for hh in range(NH):
                    OP("sp", lambda e, hh=hh: e.dma_start(out=kts_scr[:, hh, :], in_=KTt[:, hh, :NS]),
                       reads=[r_KTt], writes=[r_sscr], dma=True)
                for hh in range(NH):
                    OP("sp", lambda e, hh=hh: e.dma_start(out=vs_scr[:, hh * HD:(hh + 1) * HD], in_=vb[:NS, hh, 0, :]),
                       reads=[r_vb], writes=[r_sscr], dma=True)
            else:
                ti = tok0 // TN
                for hh in range(NH):
                    OP("sp", lambda e, hh=hh: e.dma_start(out=kt_scr[:, hh, tok0:tok0 + N], in_=KTt[:, hh, :N]),
                       reads=[r_KTt], writes=[r_ktscr[ti]], dma=True)
                for hh in range(NH):
                    OP("sp", lambda e, hh=hh: e.dma_start(
                        out=v_scr.rearrange("p h j d -> p h (j d)")[:, hh, tok0:tok0 + N],
                        in_=vb[:, hh, :, :].rearrange("p t d -> p (t d)")),
                       reads=[r_vb], writes=[r_vscr[ti]], dma=True)
            ck("prior tile done")
            if not full:
                return

            if kind == "sample":
                OP("sp", lambda e: e.dma_start(out=xs_b[0][:15, :PW], in_=sp_in[:, :]),
                   reads=[], writes=[r_xs[0]], dma=True)
                b = acq()
                bv = banks[b][:, :].rearrange("p (c n) -> p c n", c=8)
                for c in range(8):
                    OP("pe", lambda e, c=c, bv=bv: e.transpose(
                        out=bv[:, c, 0:16], in_=xs_b[0][:16, c * 128:(c + 1) * 128], identity=identf[:16, :16]),
                       reads=[r_xs[0], r_const], writes=[r_bank[b]])
                OP("dve", lambda e, bv=bv: e.tensor_copy(uT[:, :, 1:16], bv[:, :, 0:15]),
                   reads=[r_bank[b]], writes=[r_uT])
                rel(b)

            ck("A done")
            L0 = 16 + N
            if kind == "own":
                OP("dve", lambda e: e.tensor_copy(uT[:, :, 0:16], uhist[:, :, :]), reads=[r_uhist], writes=[r_uT])
            for gi, w in enumerate(WIN):
                c0 = 2 * gi
                cur, rcur = uT[:, c0:c0 + 2, :], r_uT
                lo = 0
                bufs = [(tA, r_tA), (tB, r_tB)]
                for k in range(gi + 1):
                    sh = 1 << k
                    lo += sh
                    dstt, rd = bufs[k % 2]
                    OP("dve", lambda e, cur=cur, dstt=dstt, lo=lo, sh=sh: e.tensor_tensor(
                        dstt[:, :, lo:L0], cur[:, :, lo:L0], cur[:, :, lo - sh:L0 - sh], ALU.add),
                       reads=[rcur], writes=[rd])
                    cur, rcur = dstt, rd
                OP("dve", lambda e, cur=cur, c0=c0, w=w: e.scalar_tensor_tensor(
                    diffT[:, c0:c0 + 2, :N], cur[:, :, 16:16 + N], 1.0 / w, uT[:, c0:c0 + 2, 16:16 + N],
                    ALU.mult, ALU.subtract),
                   reads=[rcur, r_uT], writes=[r_diff[gi]])
                if kind == "own" and g == 0:
                    for cc in range(2):
                        OP("dve", lambda e, cur=cur, cc=cc, gi=gi: e.tensor_tensor(
                            cur[:, cc, 16:32], cur[:, cc, 16:32], cntinv[:, gi, :], ALU.mult),
                           reads=[rcur, r_const], writes=[rcur])
                        OP("dve", lambda e, cur=cur, cc=cc, c0=c0: e.tensor_tensor(
                            diffT[:, c0 + cc, 0:16], cur[:, cc, 16:32], uT[:, c0 + cc, 16:32], ALU.subtract),
                           reads=[rcur, r_uT], writes=[r_diff[gi]])
                for ecl in range(2):
                    b = acq()
                    for cc in range(2):
                        OP("pe", lambda e, b=b, gi=gi, cc=cc, ecl=ecl, c0=c0: e.matmul(
                            banks[b][:, :N], wpool[:, gi, cc, ecl * 128:(ecl + 1) * 128], diffT[:, c0 + cc, :N],
                            start=(cc == 0), stop=(cc == 1)),
                           reads=[r_const, r_diff[gi]], writes=[r_bank[b]])
                    OP("dve", lambda e, b=b, c0=c0, ecl=ecl: e.tensor_scalar(
                        oaT[:, c0 + ecl, :N], banks[b][:, :N], spool[:, c0 + ecl:c0 + ecl + 1], None, ALU.mult),
                       reads=[r_bank[b], r_const], writes=[r_oaT[c0 + ecl]])
                    rel(b)
            if kind == "own" and not last_own:
                OP("dve", lambda e: e.tensor_copy(uhist[:, :, :], uT[:, :, N:N + 16]), reads=[r_uT], writes=[r_uhist])

            ck("B done")
            alias(XC, XA)
            if kind == "own":
                nkb = (tok0 + N) // 128
                kblocks = [(j, 128, (j - (nkb - NB)) if j >= nkb - NB else None) for j in range(nkb)]
            else:
                nkb = past_blks + 1
                kblocks = [(j, 128, None) for j in range(past_blks)] + [(past_blks, NS, 0)]

            def load_ctx(hh):
                hb = hh % 2
                if kind == "own":
                    ntile = (tok0 + N) // TN
                    OP("sp", lambda e: e.dma_start(out=KTh[hb][:, :nkb * 128], in_=kt_scr[:, hh, 0:nkb * 128]),
                       reads=r_ktscr[:ntile], writes=[r_KTh[hb]], dma=True)
                    OP("sp", lambda e: e.dma_start(out=Vh[hb][:, :nkb, :], in_=v_scr[:, hh, 0:nkb, :]),
                       reads=r_vscr[:ntile], writes=[r_Vh[hb]], dma=True)
                else:
                    OP("pool", lambda e: e.dma_start(
                        out=ckst[:, :past_blks, :], in_=ck_in[:, hh * HD:(hh + 1) * HD].rearrange("(j p) d -> p j d", p=128)),
                       reads=[], writes=[r_ckst], dma=True)
                    OP("pool", lambda e: e.dma_start(
                        out=Vh[hb][:, :past_blks, :], in_=cv_in[:, hh * HD:(hh + 1) * HD].rearrange("(j p) d -> p j d", p=128)),
                       reads=[], writes=[r_Vh[hb]], dma=True)
                    OP("sp", lambda e: e.dma_start(out=Vh[hb][:NS, past_blks, :], in_=vs_scr[:, hh * HD:(hh + 1) * HD]),
                       reads=[r_sscr], writes=[r_Vh[hb]], dma=True)
                    OP("sp", lambda e: e.dma_start(out=KTh[hb][:, PAST:PAST + NS], in_=kts_scr[:, hh, :]),
                       reads=[r_sscr], writes=[r_KTh[hb]], dma=True)
                    j0 = 0
                    while j0 < past_blks:
                        nj = min(8, past_blks - j0)
                        b = acq()
                        bb = banks[b][:, :].bitcast(BF16)
                        for jj in range(nj):
                            OP("pe", lambda e, jj=jj, j0=j0, bb=bb: e.transpose(
                                out=bb[:, jj * 128:(jj + 1) * 128], in_=ckst[:, j0 + jj, :], identity=ident[:, :]),
                               reads=[r_ckst, r_const], writes=[r_bank[b]])
                        OP("dve", lambda e, j0=j0, nj=nj, bb=bb: e.tensor_copy(
                            KTh[hb][:, j0 * 128:(j0 + nj) * 128], bb[:, :nj * 128]),
                           reads=[r_bank[b]], writes=[r_KTh[hb]])
                        rel(b)
                        j0 += nj

            NBF = 3
            units = []
            for hh in range(NH):
                rb = list(reversed(kblocks))
                for idx, (j, kp, jj) in enumerate(rb):
                    units.append((hh, j, kp, jj, idx == 0, idx == len(rb) - 1))
            ubank = {}
            obank = {}

            def stage1(n):
                hh, j, kp, jj, first, last = units[n]
                hb, i3 = hh % 2, n % NBF
                kslice = KTh[hb][:, j * 128:j * 128 + kp]
                z1 = acq()
                OP("pe", lambda e: e.matmul(banks[z1][:kp, :N], kslice, QT[:, hh, :N], start=True, stop=True),
                   reads=[r_KTh[hb], r_QT[hh]], writes=[r_bank[z1]])
                OP("act", lambda e: e.activation(out=Eb[i3][:kp, :N], in_=banks[z1][:kp, :N], func=AF.Exp),
                   reads=[r_bank[z1]], writes=[r_E[i3]])
                rel(z1)
                OP("act", lambda e: e.activation(out=Lb[i3][:kp, :N], in_=Eb[i3][:kp, :N], func=AF.Ln, bias=1.0),
                   reads=[r_E[i3]], writes=[r_L[i3]])
                if jj is not None:
                    OP("dve", lambda e: e.tensor_tensor(Lb[i3][:kp, :N], Lb[i3][:kp, :N], masks[:kp, jj, :N], ALU.mult),
                       reads=[r_L[i3], r_const], writes=[r_L[i3]])

            def stage2(n):
                hh, j, kp, jj, first, last = units[n]
                hb, i3, ip = hh % 2, n % NBF, (n - 1) % NBF
                kslice = KTh[hb][:, j * 128:j * 128 + kp]
                z2 = acq()
                OP("pe", lambda e: e.matmul(banks[z2][:kp, :N], kslice, QT[:, hh, :N], start=True, stop=False),
                   reads=[r_KTh[hb], r_QT[hh]], writes=[r_bank[z2]])
                OP("pe", lambda e: e.matmul(banks[z2][:kp, :N], negtri[:kp, :kp], Lb[i3][:kp, :N], start=False, stop=first),
                   reads=[r_const, r_L[i3]], writes=[r_bank[z2]])
                if not first:
                    OP("pe", lambda e: e.matmul(banks[z2][:kp, :N], negones[:, :kp], Sbb[ip][:, :N], start=False, stop=True),
                       reads=[r_const, r_Sb[ip]], writes=[r_bank[z2]])
                OP("act", lambda e: e.activation(out=Ab[i3][:kp, :N], in_=banks[z2][:kp, :N], func=AF.Exp),
                   reads=[r_bank[z2]], writes=[r_A[i3]])
                rel(z2)
                if jj is not None:
                    OP("dve", lambda e: e.tensor_tensor(Ab[i3][:kp, :N], Ab[i3][:kp, :N], masks[:kp, jj, :N], ALU.mult),
                       reads=[r_A[i3], r_const], writes=[r_A[i3]])
                if first:
                    OP("dve", lambda e: e.memset(S32[:, :N], 0.0), reads=[], writes=[r_S32])
                if not last:
                    OP("dve", lambda e: e.tensor_tensor(S32[:kp, :N], S32[:kp, :N], Lb[i3][:kp, :N], ALU.add),
                       reads=[r_S32, r_L[i3]], writes=[r_S32])
                    OP("dve", lambda e: e.tensor_copy(Sbb[i3][:, :N], S32[:, :N]),
                       reads=[r_S32], writes=[r_Sb[i3]])

            def stage3(n):
                hh, j, kp, jj, first, last = units[n]
                hb, i3 = hh % 2, n % NBF
                if first:
                    obank[hh] = acq()
                bo = obank[hh]
                OP("pe", lambda e: e.matmul(banks[bo][:, :N], Vh[hb][:kp, j, :], Ab[i3][:kp, :N], start=first, stop=last),
                   reads=[r_Vh[hb], r_A[i3]], writes=[r_bank[bo]])
                if last:
                    OP("dve", lambda e: e.tensor_copy(obT[:, hh, :N], banks[bo][:, :N]),
                       reads=[r_bank[bo]], writes=[r_obT[hh]])
                    rel(bo)
                    if hh + 2 < NH:
                        load_ctx(hh + 2)

            load_ctx(0)
            load_ctx(1)
            nu = len(units)
            for it in range(nu + 2):
                if it < nu:
                    stage1(it)
                if 0 <= it - 1 < nu:
                    stage2(it - 1)
                if 0 <= it - 2 < nu:
                    stage3(it - 2)

            ck("C done")
            alias(XD, XC + XA)
            for tb in range(NB):
                OP("sp", lambda e, tb=tb: e.dma_start(out=ht[:pn, tb, :], in_=x_rows[tb * 128:tb * 128 + pn, :]),
                   reads=[], writes=[r_ht[tb]], dma=True)
            for p in range(4):
                wb, rwb = wget(wsrc(w_branch, 0, D, 512 * p, 512 * p + 512), DC)
                wga, rwga = wget(wsrc(w_in, 0, D, 4 * PW + 512 * p, 4 * PW + 512 * p + 512), DC)
                bybs = []
                for mcl in range(4):
                    bya = proj_fm(wb, rwb, mcl, 8, lambda kc: oaT[:, kc, :N], r_oaT, N)
                    bga = proj_fm(wga, rwga, mcl, DC, lambda kc: xnT[:, kc, :N], r_xnT, N)
                    byb = proj_fm(wb, rwb, mcl, 8, lambda kc: obT[:, kc, :N], r_obT, N, kofs=8)
                    bybs.append(byb)
                    OP("act", lambda e, bga=bga: e.activation(out=sAb[0][:, :N], in_=banks[bga][:, :N], func=AF.Sigmoid),
                       reads=[r_bank[bga]], writes=[r_sA[0]])
                    OP("dve", lambda e, bya=bya, mcl=mcl: e.tensor_tensor(t1b[mcl][:, :N], banks[bya][:, :N], sAb[0][:, :N], ALU.mult),
                       reads=[r_bank[bya], r_sA[0]], writes=[r_t1[mcl]])
                    rel(bya)
                    rel(bga)
                wrel()
                wrel()
                wgb, rwgb = wget(wsrc(w_in, 0, D, 4 * PW + D + 512 * p, 4 * PW + D + 512 * p + 512), DC)
                for mcl in range(4):
                    mc = 4 * p + mcl
                    byb = bybs[mcl]
                    bgb = proj_fm(wgb, rwgb, mcl, DC, lambda kc: xnT[:, kc, :N], r_xnT, N)
                    OP("act", lambda e, bgb=bgb: e.activation(out=sBb[0][:, :N], in_=banks[bgb][:, :N], func=AF.Sigmoid),
                       reads=[r_bank[bgb]], writes=[r_sB[0]])
                    OP("dve", lambda e, byb=byb: e.tensor_tensor(sBb[0][:, :N], banks[byb][:, :N], sBb[0][:, :N], ALU.mult),
                       reads=[r_bank[byb], r_sB[0]], writes=[r_sB[0]])
                    OP("dve", lambda e, mc=mc, mcl=mcl: e.tensor_tensor(mergedT[:, mc, :N], t1b[mcl][:, :N], sBb[0][:, :N], ALU.add),
                       reads=[r_t1[mcl], r_sB[0]], writes=[r_mer[mc]])
                    rel(bgb)
                    rel(byb)
                wrel()

            ck("D done")
            for cg in range(4):
                wsl, rw = wget(wsrc(w_out, 0, D, 512 * cg, 512 * cg + 512), DC)
                for tb in range(NB):
                    b = acq()
                    for kc in range(DC):
                        OP("pe", lambda e, kc=kc, b=b, tb=tb, wsl=wsl: e.matmul(
                            banks[b][:pn, :512], mergedT[:, kc, tb * 128:tb * 128 + pn], wsl[:, kc, :512],
                            start=(kc == 0), stop=(kc == DC - 1)),
                           reads=[rw, r_mer[kc]], writes=[r_bank[b]])
                    OP("dve", lambda e, b=b, tb=tb, cg=cg: e.tensor_tensor(
                        ht[:pn, tb, 512 * cg:512 * cg + 512], banks[b][:pn, :512], ht[:pn, tb, 512 * cg:512 * cg + 512], ALU.add),
                       reads=[r_bank[b], r_ht[tb]], writes=[r_ht[tb]])
                    rel(b)
                wrel()

            ck("E done")
            alias(r_n2t, r_mer)
            for tb in range(NB):
                rmsnorm_rows(ht[:pn, tb, :], r_ht[tb], pn, n2t[:pn, tb, :], r_n2t[tb], tb)
            transposes_to_T(n2t, r_n2t, NB, pn, N, gffn)

            ck("F done")
            alias(r_hid, r_n2t + r_mer)
            gi_ = 0
            for pp in range(FC // 4):
                wg, rwg = wget(wsrc(w_gu, 0, D, 512 * pp, 512 * pp + 512), DC)
                wu, rwu = wget(wsrc(w_gu, 0, D, DFF + 512 * pp, DFF + 512 * pp + 512), DC)
                for fcl in range(4):
                    fc = 4 * pp + fcl
                    i2 = gi_ % 2
                    gi_ += 1
                    bg = proj_fm(wg, rwg, fcl, DC, lambda kc: xnT[:, kc, :N], r_xnT, N)
                    bu = proj_fm(wu, rwu, fcl, DC, lambda kc: xnT[:, kc, :N], r_xnT, N)
                    OP("act", lambda e, bg=bg, i2=i2: e.activation(out=t1b[i2][:, :N], in_=banks[bg][:, :N], func=AF.Silu),
                       reads=[r_bank[bg]], writes=[r_t1[i2]])
                    OP("dve", lambda e, bu=bu, i2=i2, fc=fc: e.tensor_tensor(hidT[:, fc, :N], banks[bu][:, :N], t1b[i2][:, :N], ALU.mult),
                       reads=[r_bank[bu], r_t1[i2]], writes=[r_hid[fc]])
                    rel(bg)
                    rel(bu)
                wrel()
                wrel()

            ck("G done")
            kparts = [(0, 16), (16, 32), (32, FC)]
            for cg in range(4):
                bs = [acq() for _ in range(NB)]
                for (k0, k1) in kparts:
                    wsl, rw = wget(wsrc(w_down, k0 * 128, k1 * 128, 512 * cg, 512 * cg + 512), k1 - k0)
                    for tb in range(NB):
                        for kc in range(k0, k1):
                            OP("pe", lambda e, kc=kc, k0=k0, tb=tb, wsl=wsl, b=bs[tb]: e.matmul(
                                banks[b][:pn, :512], hidT[:, kc, tb * 128:tb * 128 + pn], wsl[:, kc - k0, :512],
                                start=(kc == 0), stop=(kc == FC - 1)),
                               reads=[rw, r_hid[kc]], writes=[r_bank[bs[tb]]])
                    wrel()
                for tb in range(NB):
                    OP("dve", lambda e, b=bs[tb], tb=tb, cg=cg: e.tensor_tensor(
                        ht[:pn, tb, 512 * cg:512 * cg + 512], banks[b][:pn, :512], ht[:pn, tb, 512 * cg:512 * cg + 512], ALU.add),
                       reads=[r_bank[bs[tb]], r_ht[tb]], writes=[r_ht[tb]])
                    rel(bs[tb])

            ck("H done")
            alias(r_n2t, r_hid)
            yout = y_own if kind == "own" else y_s
            for tb in range(NB):
                rmsnorm_rows(ht[:pn, tb, :], r_ht[tb], pn, ht[:pn, tb, :], r_ht[tb], tb, gscale=gfin[:pn, :])
                r0 = (g * TN if kind == "own" else 0) + tb * 128
                OP("sp", lambda e, tb=tb, r0=r0: e.dma_start(out=yout[r0:r0 + pn, :], in_=ht[:pn, tb, :]),
                   reads=[r_ht[tb]], writes=[r_out], dma=True)

        try:
            ck("consts")
            for g in range(n_prior_tiles):
                tile("prior", g, xp_pri[g * TN:(g + 1) * TN, :], TN, g * TN)
            for g in range(n_own_tiles):
                tile("own", g, xp_own[g * TN:(g + 1) * TN, :], TN, NPRI + g * TN)
                ck("own tile done")
            tile("sample", 0, xs_in, NS, 0)
        except _Stop:
            pass
        return wreq

    for r in R.values():
        r.last_w, r.readers = None, []
    wpid, wconv, wscr, r_wscr = {}, {}, None, []
    wlist = flow(Prog(dry=True), None)
    for key, src, kcn in wlist:
        if key not in wpid:
            wpid[key] = len(wpid)
            wconv[key] = (src, kcn)
    wscr = nc.dram_tensor("wscr", [len(wpid), 128, 16, 512], BF16, kind="Internal").ap()
    r_wscr = [Res(f"wscr{i}") for i in range(len(wpid))]
    prog = Prog()
    flow(prog, wlist)

    for e in ("pe", "act", "dve", "pool"):
        n = 0
        for o in prog.streams[e]:
            if o.dma:
                continue
            if o.signaled:
                o.sig = (sems_eng[e][(n // EPOCH) % 4], n % EPOCH + 1)
                n += 1
        assert n < 4 * EPOCH, (e, n)
    nd = 0
    dma_prev = {}
    for e in ("pool", "sp"):
        pass
    allops = []
    for e in Prog.ENGS:
        for o in prog.streams[e]:
            if o.dma:
                allops.append(o)
    qsems = {"pool": sems_dma[:NDL // 2], "sp": sems_dma[NDL // 2:]}
    for e in ("pool", "sp"):
        k = 0
        ss = qsems[e]
        for o in prog.streams[e]:
            if not o.dma:
                continue
            s = ss[k % len(ss)]
            val = 16 * (k // len(ss) + 1)
            o.sig = (s, val)
            o.idx = (s, val - 16)
            k += 1

    def emit_stream(ename, eng):
        waited = {}
        for o in prog.streams[ename]:
            need = {}
            if o.dma and o.idx[1] > 0:
                need[id(o.idx[0])] = (o.idx[0], o.idx[1])
            for d in o.deps:
                s, v = d.sig
                k = id(s)
                if k not in need or need[k][1] < v:
                    need[k] = (s, v)
            for k, (s, v) in need.items():
                if waited.get(k, 0) >= v:
                    continue
                eng.wait_ge(s, v)
                waited[k] = v
            ins = o.fn(eng)
            if o.dma:
                ins.then_inc(o.sig[0], 16)
            elif o.signaled:
                ins.then_inc(o.sig[0], 1)
        if ename in ("sp", "pool"):
            last = {}
            for o in prog.streams[ename]:
                if o.dma:
                    last[id(o.sig[0])] = o.sig
            for s, v in last.values():
                eng.wait_ge(s, v)

    with nc.Block() as block:
        @block.tensor
        def _(e):
            emit_stream("pe", e)

        @block.scalar
        def _(e):
            emit_stream("act", e)

        @block.vector
        def _(e):
            emit_stream("dve", e)

        @block.gpsimd
        def _(e):
            emit_stream("pool", e)

        @block.sync
        def _(e):
            emit_stream("sp", e)
    es.close()
    return nc


def _consts(h_is_first):
    ident = np.eye(128, dtype=np.float32)
    s_ = np.arange(128)
    negtri = -(s_[:, None] >= s_[None, :]).astype(np.float32)
    negones = -np.ones((128, 128), np.float32)
    t_ = np.arange(TN)
    masks = np.stack([((128 * jj + s_)[:, None] < t_[None, :]).astype(np.float32) for jj in range(4)], axis=1)
    cnt = np.empty((4, 16), np.float32)
    for gi, w in enumerate(WIN):
        if h_is_first:
            cnt[gi] = 1.0 / np.minimum(np.arange(16) + 1, w)
        else:
            cnt[gi] = 1.0 / w
    cnt = np.ascontiguousarray(np.broadcast_to(cnt.reshape(1, 64), (128, 64)))
    return ident, negtri, negones, np.ascontiguousarray(masks.reshape(128, 4 * TN)), cnt


def _run(inputs, n_own_tiles, n_prior_tiles, past_blks, batch, dec_batch):
    f = lambda a: np.ascontiguousarray(np.asarray(a, dtype=np.float32))
    x_prompt, x_sample = f(inputs["x_prompt"]), f(inputs["x_sample"])
    cache_k, cache_v, state_pool = f(inputs["cache_k"]), f(inputs["cache_v"]), f(inputs["state_pool"])
    NOWN, NPRI = n_own_tiles * TN, n_prior_tiles * TN
    seq = x_prompt.shape[1]
    assert seq == NOWN + NPRI and NOWN == NPRI
    past = past_blks * 128
    nc = build(n_own_tiles, n_prior_tiles, past_blks)

    def vec16(v):
        return np.ascontiguousarray(f(v).reshape(-1, 128).T)

    shared = {
        "w_in": f(inputs["w_in"][0]), "w_pool": f(inputs["w_pool"][0]), "w_branch": f(inputs["w_branch"][0]),
        "w_out": f(inputs["w_out"][0]), "w_gu": f(inputs["w_gate_up"][0]), "w_down": f(inputs["w_down"][0]),
        "gmix": vec16(inputs["g_mix"][0]), "gffn": vec16(inputs["g_ffn"][0]), "spool": vec16(inputs["s_pool"][0]),
        "gfin": np.ascontiguousarray(np.broadcast_to(f(inputs["g_final"]).reshape(1, D), (128, D))),
    }
    in_maps = []
    ncores = 2 * batch
    assert ncores == dec_batch
    for c in range(ncores):
        b, h = c // 2, c % 2
        ident, negtri, negones, masks, cnt = _consts(h == 0)
        m = dict(shared)
        m["xp_own"] = np.ascontiguousarray(x_prompt[b, h * NOWN:(h + 1) * NOWN])
        m["xp_pri"] = np.ascontiguousarray(x_prompt[b, :NPRI]) if h == 1 else np.zeros((NPRI, D), np.float32)
        m["xs_in"] = np.ascontiguousarray(x_sample[c])
        m["ck_in"] = np.ascontiguousarray(cache_k[0, c].reshape(past, NH * HD))
        m["cv_in"] = np.ascontiguousarray(cache_v[0, c].reshape(past, NH * HD))
        m["sp_in"] = np.ascontiguousarray(state_pool[0, c])
        m.update(ident=ident, negtri=negtri, negones=negones, masks=masks, cntinv=cnt)
        in_maps.append(m)
    res = run_bass_kernel_spmd(nc, in_maps, core_ids=list(range(ncores)))
    rs = res.results
    y_prompt = np.empty((batch, seq, D), np.float32)
    nkp = np.empty((1, batch, seq, NH, HD), np.float32)
    nvp = np.empty((1, batch, seq, NH, HD), np.float32)
    npp = np.empty((1, batch, 15, PW), np.float32)
    y_sample = np.empty((dec_batch, NS, D), np.float32)
    nks = np.empty((1, dec_batch, NS, NH, HD), np.float32)
    nvs = np.empty((1, dec_batch, NS, NH, HD), np.float32)
    nps = np.empty((1, dec_batch, 15, PW), np.float32)
    for c in range(ncores):
        b, h = c // 2, c % 2
        r = rs[c]
        y_prompt[b, h * NOWN:(h + 1) * NOWN] = r["y_own"]
        nkp[0, b, h * NOWN:(h + 1) * NOWN] = r["nk_own"].reshape(NOWN, NH, HD)
        nvp[0, b, h * NOWN:(h + 1) * NOWN] = r["nv_own"].reshape(NOWN, NH, HD)
        if h == 1:
            npp[0, b] = r["pool_p"]
        y_sample[c] = r["y_s"]
        nks[0, c] = r["nk_s"].reshape(NS, NH, HD)
        nvs[0, c] = r["nv_s"].reshape(NS, NH, HD)
        nps[0, c] = r["pool_s"]
    return (y_prompt, y_sample, nkp, nvp, npp, nks, nvs, nps)


def kernel(**inputs):
    return _run(inputs, 4, 4, 32, 4, 8)
```

```python
import os
import numpy as np
import concourse.bass as bass
import concourse.mybir as mybir
from concourse.bass_utils import run_bass_kernel_spmd

F32 = mybir.dt.float32
BF16 = mybir.dt.bfloat16
AF = mybir.ActivationFunctionType
ALU = mybir.AluOpType

D = 2048
DC = 16
NH = 8
HD = 128
PW = 1024
DFF = 5632
FC = 44
INW = 8192
NS = 16
EPS = 1e-6
TN = 512
WIN = (2, 4, 8, 16)
NCORES = 8
WRING = 3
EPOCH = 24000


class _Stop(Exception):
    pass


class Res:
    __slots__ = ("last_w", "readers", "name")

    def __init__(self, name=""):
        self.last_w = None
        self.readers = []
        self.name = name


class Op:
    __slots__ = ("eng", "fn", "deps", "signaled", "sig", "dma", "idx")

    def __init__(self, eng, fn, dma):
        self.eng = eng
        self.fn = fn
        self.deps = ()
        self.signaled = False
        self.sig = None
        self.dma = dma


class Prog:
    ENGS = ("pe", "act", "dve", "pool", "sp")

    def __init__(self, dry=False):
        self.dry = dry
        self.streams = {e: [] for e in self.ENGS}

    def op(self, eng, fn, reads=(), writes=(), dma=False):
        if self.dry:
            return None
        o = Op(eng, fn, dma)
        deps = {}
        for r in reads:
            w = r.last_w
            if w is not None:
                deps[id(w)] = w
        for wr in writes:
            w = wr.last_w
            if w is not None:
                deps[id(w)] = w
            for rd in wr.readers:
                deps[id(rd)] = rd
        if eng == "pe" and not dma:
            deps = {k: v for k, v in deps.items() if v.dma or v.eng != "pe"}
        o.deps = tuple(deps.values())
        for d in o.deps:
            d.signaled = True
        for r in reads:
            r.readers.append(o)
        for wr in writes:
            wr.last_w = o
            wr.readers = []
        self.streams[eng].append(o)
        return o


def alias(new, old):
    pend = []
    for r in old:
        if r.last_w is not None:
            pend.append(r.last_w)
        pend.extend(r.readers)
    if not pend:
        return
    for r in new:
        r.readers = r.readers + pend


def build(n_own_tiles, n_prior_tiles, past_blks):
    NOWN = n_own_tiles * TN
    NPRI = n_prior_tiles * TN
    SEQL = NOWN + NPRI
    NBLK = SEQL // 128
    PAST = past_blks * 128
    MAXKB = max(NBLK, past_blks + 1)

    nc = bass.Bass("TRN2", target_bir_lowering=False)

    def din(name, shape, dt=F32):
        return nc.dram_tensor(name, list(shape), dt, kind="ExternalInput").ap()

    def dout(name, shape, dt=F32):
        return nc.dram_tensor(name, list(shape), dt, kind="ExternalOutput").ap()

    xp_own = din("xp_own", [NOWN, D])
    xp_pri = din("xp_pri", [NPRI, D])
    xs_in = din("xs_in", [NS, D])
    ck_in = din("ck_in", [PAST, NH * HD])
    cv_in = din("cv_in", [PAST, NH * HD])
    sp_in = din("sp_in", [15, PW])
    w_in = din("w_in", [D, INW])
    w_pool = din("w_pool", [4, 256, 256])
    w_branch = din("w_branch", [D, D])
    w_out = din("w_out", [D, D])
    w_gu = din("w_gu", [D, 2 * DFF])
    w_down = din("w_down", [DFF, D])
    gmix_in = din("gmix", [128, DC])
    gffn_in = din("gffn", [128, DC])
    spool_in = din("spool", [128, 8])
    gfin_in = din("gfin", [128, D])
    cnt_in = din("cntinv", [128, 4 * 16])
    ident_in = din("ident", [128, 128])
    negtri_in = din("negtri", [128, 128])
    negones_in = din("negones", [128, 128])
    masks_in = din("masks", [128, 4 * TN])

    y_own = dout("y_own", [NOWN, D])
    nk_own = dout("nk_own", [NOWN, NH * HD])
    nv_own = dout("nv_own", [NOWN, NH * HD])
    pool_p = dout("pool_p", [15, PW])
    y_s = dout("y_s", [NS, D])
    nk_s = dout("nk_s", [NS, NH * HD])
    nv_s = dout("nv_s", [NS, NH * HD])
    pool_s = dout("pool_s", [15, PW])

    kt_scr = nc.dram_tensor("kt_scr", [128, NH, SEQL], BF16, kind="Internal").ap()
    v_scr = nc.dram_tensor("v_scr", [128, NH, NBLK, HD], BF16, kind="Internal").ap()
    kts_scr = nc.dram_tensor("kts_scr", [128, NH, NS], BF16, kind="Internal").ap()
    vs_scr = nc.dram_tensor("vs_scr", [NS, NH * HD], BF16, kind="Internal").ap()

    XBYTES = 99072
    from contextlib import ExitStack
    es = ExitStack()

    def sb(name, shape, dt):
        return es.enter_context(nc.sbuf_tensor(name, list(shape), dt))

    X = sb("X", [128, XBYTES // 4], F32)
    wring = [sb(f"wr{i}", [128, 16, 512], BF16) for i in range(WRING)]
    xnT = sb("xnT", [128, DC, TN], BF16)
    QT = sb("QT", [128, NH, TN], BF16)
    oaT = sb("oaT", [128, 8, TN], BF16)
    obT = sb("obT", [128, NH, TN], BF16)
    ident = sb("identb", [128, 128], BF16)
    negtri = sb("negtrib", [128, 128], BF16)
    negones = sb("negonesb", [128, 128], BF16)
    masks = sb("masksb", [128, 4, TN], BF16)
    identf = sb("identf", [128, 128], F32)
    gmix = sb("gmixs", [128, DC], F32)
    gffn = sb("gffns", [128, DC], F32)
    spool = sb("spools", [128, 8], F32)
    gfin = sb("gfins", [128, D], F32)
    cntinv = sb("cntinvs", [128, 4, 16], F32)
    wpool = sb("wpools", [128, 4, 2, 256], BF16)
    stats = sb("stats", [128, 16], F32)
    utok = sb("utok", [NS, PW], F32)
    uhist = sb("uhist", [128, 8, 16], F32)

    def xv(off, nbytes, dt, pat=None, **kw):
        a = X[:, off // 4:(off + nbytes) // 4]
        if dt == BF16:
            a = a.bitcast(BF16)
        if pat:
            a = a.rearrange(pat, **kw)
        return a

    xs_b = [xv(0, 8192, F32), xv(8192, 8192, F32)]
    xnt = xv(16384, 16384, BF16, "p (t f) -> p t f", t=4)
    UW = 16 + TN
    uT = xv(32768, 8 * UW * 4, F32, "p (c n) -> p c n", c=8)
    o = 32768 + 8 * UW * 4
    tA = xv(o, 2 * UW * 4, F32, "p (c n) -> p c n", c=2)
    o += 2 * UW * 4
    tB = xv(o, 2 * UW * 4, F32, "p (c n) -> p c n", c=2)
    o += 2 * UW * 4
    diffT = xv(o, 8192, BF16, "p (c n) -> p c n", c=8)
    o += 8192
    kvs = [xv(o + i * 2048, 2048, F32) for i in range(4)]
    o += 8192
    kb = xv(o, 8192, BF16, "p (t f) -> p t f", t=4)
    o += 8192
    vb = xv(o, 8192, BF16, "p (h t d) -> p h t d", h=NH, t=4)
    o += 8192
    KTt = xv(o, 8192, BF16, "p (c n) -> p c n", c=8)
    o += 8192
    assert o <= XBYTES, o
    xnt_alt = xv(32768, 16384, BF16, "p (t f) -> p t f", t=4)
    xnT_alt = xv(49152, 16384, BF16, "p (c n) -> p c n", c=DC)
    KB2 = MAXKB * 128 * 2
    assert KB2 <= 9216
    KTh = [xv(0, 9216, BF16), xv(9216, 9216, BF16)]
    Vh = [xv(18432, 9216, BF16, "p (j d) -> p j d", d=HD), xv(27648, 9216, BF16, "p (j d) -> p j d", d=HD)]
    Eb = [xv(36864 + 2048 * i, 2048, F32) for i in range(3)]
    Lb = [xv(43008 + 1024 * i, 1024, BF16) for i in range(3)]
    Ab = [xv(46080 + 1024 * i, 1024, BF16) for i in range(3)]
    S32 = xv(49152, 2048, F32)
    Sbb = [xv(51200 + 1024 * i, 1024, BF16) for i in range(3)]
    ckst = xv(54272, 8192, BF16, "p (j d) -> p j d", d=HD)
    ht = xv(0, 32768, F32, "p (t f) -> p t f", t=4)
    hidT = xv(32768, 45056, BF16, "p (c n) -> p c n", c=FC)
    mergedT = xv(32768, 16384, BF16, "p (c n) -> p c n", c=DC)
    n2t = xv(32768, 16384, BF16, "p (t f) -> p t f", t=4)
    t1b = [xv(77824 + 2048 * i, 2048, F32) for i in range(4)]
    sAb = [xv(86016, 2048, F32)]
    sBb = [xv(88064, 2048, F32)]

    banks = [es.enter_context(nc.psum_tensor(f"bank{i}", [128, 512], F32)) for i in range(8)]

    NDL = 20
    sems_eng = {e: [es.enter_context(nc.semaphore(f"s_{e}{k}")) for k in range(4)] for e in ("pe", "act", "dve", "pool")}
    sems_dma = [es.enter_context(nc.semaphore(f"s_dma{k}")) for k in range(NDL)]

    R = {}

    def res(name):
        if name not in R:
            R[name] = Res(name)
        return R[name]

    r_xs = [res("xs0"), res("xs1")]
    r_xnt = [res(f"xnt{t}") for t in range(4)]
    r_xnT = [res(f"xnT{k}") for k in range(DC)]
    r_uT = res("uT")
    r_xnt_alt = [res(f"xntalt{t}") for t in range(4)]
    r_xnT_alt = [res(f"xnTalt{k}") for k in range(DC)]
    r_tA, r_tB = res("tA"), res("tB")
    r_diff = [res(f"diff{g}") for g in range(4)]
    r_kvs = [res(f"kvs{i}") for i in range(4)]
    r_kb, r_vb, r_KTt = res("kb"), res("vb"), res("KTt")
    r_QT = [res(f"QT{h}") for h in range(NH)]
    r_oaT = [res(f"oaT{c}") for c in range(8)]
    r_obT = [res(f"obT{h}") for h in range(NH)]
    r_KTh = [res("KTh0"), res("KTh1")]
    r_Vh = [res("Vh0"), res("Vh1")]
    r_E = [res(f"E{i}") for i in range(3)]
    r_L = [res(f"L{i}") for i in range(3)]
    r_A = [res(f"A{i}") for i in range(3)]
    r_S32 = res("S32")
    r_Sb = [res(f"Sb{i}") for i in range(3)]
    r_ckst = res("ckst")
    r_ht = [res(f"ht{t}") for t in range(4)]
    r_hid = [res(f"hid{c}") for c in range(FC)]
    r_mer = [res(f"mer{c}") for c in range(DC)]
    r_n2t = [res(f"n2t{t}") for t in range(4)]
    r_sA = [res("sA0")]
    r_sB = [res("sB0")]
    r_t1 = [res(f"t1{i}") for i in range(4)]
    r_bank = [res(f"bank{i}") for i in range(8)]
    r_wr = [res(f"wr{i}") for i in range(WRING)]
    r_const = res("const")
    r_stats = [res(f"st{i}") for i in range(16)]
    r_utok = res("utok")
    r_uhist = res("uhist")
    r_ktscr = [res(f"ktscr{t}") for t in range(SEQL // TN)]
    r_vscr = [res(f"vscr{t}") for t in range(SEQL // TN)]
    r_sscr = res("sscr")
    r_out = res("out")

    XA = r_xs + r_xnt + [r_uT, r_tA, r_tB] + r_diff + r_kvs + [r_kb, r_vb, r_KTt]
    XALT = r_xnt_alt + r_xnT_alt
    XC = r_KTh + r_Vh + r_E + r_L + r_A + [r_S32] + r_Sb + [r_ckst]
    XD = r_ht + r_hid + r_mer + r_n2t + r_sA + r_sB + r_t1

    def flow(prog, wlist):
        wreq = []
        wstate = {"next_load": 0, "released": 0}
        last_out = []

        def OP(eng, fn, reads=(), writes=(), dma=False):
            xb = [r for r in reads if r.name.startswith("bank")]
            if xb:
                writes = list(writes) + xb
            return prog.op(eng, fn, reads, writes, dma)

        ckstate = {"n": 0}
        kstop = int(os.environ.get("KSTOP", "100000"))

        def ck(label=""):
            ckstate["n"] += 1
            if ckstate["n"] == kstop:
                print("KSTOP at", ckstate["n"], label, flush=True)
                raise _Stop()

        wseen = set()

        def wload(i):
            key, src, kcn = wlist[i]
            slot = i % WRING
            pid = wpid[key]
            if key not in wseen:
                wseen.add(key)
                OP("pool", lambda e, s=slot, src=src, k=kcn: e.dma_start(out=wring[s][:, 0:k, :], in_=src),
                   reads=[], writes=[r_wr[slot]], dma=True)
                OP("sp", lambda e, s=slot, pid=pid, k=kcn: e.dma_start(out=wscr[pid, :, 0:k, :], in_=wring[s][:, 0:k, :]),
                   reads=[r_wr[slot]], writes=[r_wscr[pid]], dma=True)
            else:
                OP("sp", lambda e, s=slot, pid=pid, k=kcn: e.dma_start(out=wring[s][:, 0:k, :], in_=wscr[pid, :, 0:k, :]),
                   reads=[r_wscr[pid]], writes=[r_wr[slot]], dma=True)

        def wfill():
            if prog.dry:
                return
            while wstate["next_load"] < min(len(wlist), wstate["released"] + WRING):
                wload(wstate["next_load"])
                wstate["next_load"] += 1

        def wget(srckey, kcn):
            i = len(wreq)
            wreq.append((srckey[0], srckey[1], kcn))
            assert i - wstate["released"] < WRING
            wfill()
            return wring[i % WRING], r_wr[i % WRING]

        def wrel():
            wstate["released"] += 1
            wfill()

        def wsrc(w, r0, r1, c0, c1):
            return ((w.tensor.name, r0, r1, c0, c1), w[r0:r1, c0:c1].rearrange("(kc p) m -> p kc m", p=128))

        free = list(range(8))

        def acq():
            return free.pop(0)

        def rel(b):
            free.append(b)

        def cload(dst, src, eng="sp"):
            OP(eng, lambda e, d=dst, s=src: e.dma_start(out=d, in_=s), reads=[], writes=[r_const], dma=True)

        cload(ident[:], ident_in, "pool")
        cload(negtri[:], negtri_in, "pool")
        cload(negones[:], negones_in, "pool")
        cload(masks[:], masks_in.rearrange("p (j t) -> p j t", j=4), "pool")
        cload(wpool[:], w_pool.rearrange("g (cc p) e -> p g cc e", p=128), "pool")
        cload(identf[:], ident_in)
        cload(gmix[:], gmix_in)
        cload(gffn[:], gffn_in)
        cload(spool[:], spool_in)
        cload(gfin[:], gfin_in)
        cload(cntinv[:], cnt_in.rearrange("p (g t) -> p g t", g=4))
        OP("dve", lambda e: e.memset(uT[:], 0.0), reads=[], writes=[r_uT])
        def rmsnorm_rows(src_ap, r_src, pn, dst_ap, r_dst, si, gscale=None):
            rs = [r_stats[si * 3 + 0], r_stats[si * 3 + 1], r_stats[si * 3 + 2]]
            c0 = si * 3
            ss, rt, rstd = stats[:pn, c0:c0 + 1], stats[:pn, c0 + 1:c0 + 2], stats[:pn, c0 + 2:c0 + 3]
            junk = dst_ap if gscale is None else None
            OP("act", lambda e: e.activation(out=junk if junk is not None else n2t[:pn, si, :], in_=src_ap, func=AF.Square, accum_out=ss),
               reads=[r_src], writes=[rs[0], r_dst if gscale is None else r_n2t[si]])
            OP("act", lambda e: e.activation(out=rt, in_=ss, func=AF.Sqrt, scale=1.0 / D, bias=EPS),
               reads=[rs[0]], writes=[rs[1]])
            OP("dve", lambda e: e.reciprocal(rstd, rt), reads=[rs[1]], writes=[rs[2]])
            if gscale is None:
                OP("dve", lambda e: e.tensor_scalar(dst_ap, src_ap, rstd, None, ALU.mult),
                   reads=[r_src, rs[2]], writes=[r_dst])
            else:
                OP("dve", lambda e: e.scalar_tensor_tensor(dst_ap, src_ap, rstd, gscale, ALU.mult, ALU.mult),
                   reads=[r_src, rs[2], r_const], writes=[r_dst])

        def transposes_to_T(src_t, r_src_t, NB, pn, N, gvec, xnT=xnT, r_xnT=r_xnT):
            for kc2 in range(DC // 2):
                b = acq()
                bb = banks[b][:, :].bitcast(BF16).rearrange("p (c n) -> p c n", c=2)
                for kk in range(2):
                    kc = kc2 * 2 + kk
                    for tb in range(NB):
                        OP("pe", lambda e, kk=kk, kc=kc, tb=tb, bb=bb: e.transpose(
                            out=bb[:, kk, tb * 128:tb * 128 + pn], in_=src_t[:pn, tb, kc * 128:(kc + 1) * 128],
                            identity=ident[:pn, :pn]),
                           reads=[r_src_t[tb], r_const], writes=[r_bank[b]])
                for kk in range(2):
                    kc = kc2 * 2 + kk
                    eng = "dve"
                    if eng == "act":
                        OP("act", lambda e, kk=kk, kc=kc, bb=bb: e.activation(
                            out=xnT[:, kc, :N], in_=bb[:, kk, :N], func=AF.Copy, scale=gvec[:, kc:kc + 1]),
                           reads=[r_bank[b], r_const], writes=[r_xnT[kc]])
                    else:
                        OP("dve", lambda e, kk=kk, kc=kc, bb=bb: e.tensor_scalar(
                            xnT[:, kc, :N], bb[:, kk, :N], gvec[:, kc:kc + 1], None, ALU.mult),
                           reads=[r_bank[b], r_const], writes=[r_xnT[kc]])
                rel(b)

        def proj_fm(wsl, r_w, mcl, kcn, rhs_fn, r_rhs, N, kofs=0):
            b = acq()
            for kc in range(kcn):
                OP("pe", lambda e, kc=kc, b=b: e.matmul(
                    banks[b][:, :N], wsl[:, kofs + kc, mcl * 128:(mcl + 1) * 128], rhs_fn(kc),
                    start=(kc == 0), stop=(kc == kcn - 1)),
                   reads=[r_w, r_rhs[kc]], writes=[r_bank[b]])
            return b

        def tile(kind, g, x_rows, N, tok0):
            NB = (N + 127) // 128
            pn = min(N, 128)
            last_prior = (kind == "prior" and g == n_prior_tiles - 1)
            last_own = (kind == "own" and g == n_own_tiles - 1)
            need_utok = last_own or kind == "sample"
            full = kind != "prior"
            use_alt = (kind == "prior" and g % 2 == 1)
            xnt_l, r_xnt_l = (xnt_alt, r_xnt_alt) if use_alt else (xnt, r_xnt)
            xnT_l, r_xnT_l = (xnT_alt, r_xnT_alt) if use_alt else (xnT, r_xnT)

            if kind == "prior":
                if use_alt:
                    alias(XALT, [r_uT, r_tA, r_tB] + r_diff + r_kvs)
            else:
                alias(XA, XD + XC + XALT)
            for tb in range(NB):
                xsb, rxs = xs_b[tb % 2], r_xs[tb % 2]
                OP("sp", lambda e, tb=tb, xsb=xsb: e.dma_start(out=xsb[:pn, :], in_=x_rows[tb * 128:tb * 128 + pn, :]),
                   reads=[], writes=[rxs], dma=True)
                rmsnorm_rows(xsb[:pn, :], rxs, pn, xnt_l[:pn, tb, :], r_xnt_l[tb], tb)
            ck("A: norm done")
            transposes_to_T(xnt_l, r_xnt_l, NB, pn, N, gmix, xnT_l, r_xnT_l)
            ck("A: transposes done")

            if full or last_prior:
                for p in range(2):
                    wsl, rw = wget(wsrc(w_in, 0, D, 512 * p, 512 * p + 512), DC)
                    if full:
                        for mcl in range(4):
                            mc = 4 * p + mcl
                            b = proj_fm(wsl, rw, mcl, DC, lambda kc: xnT_l[:, kc, :N], r_xnT_l, N)
                            OP("act", lambda e, b=b, mc=mc: e.copy(uT[:, mc, 16:16 + N], banks[b][:, :N]),
                               reads=[r_bank[b]], writes=[r_uT])
                            rel(b)
                    else:
                        for mcl in range(4):
                            mc = 4 * p + mcl
                            b = proj_fm(wsl, rw, mcl, DC, lambda kc: xnT_l[:, kc, N - 16:N], r_xnT_l, 16)
                            OP("act", lambda e, b=b, mc=mc: e.copy(uhist[:, mc, :], banks[b][:, :16]),
                               reads=[r_bank[b]], writes=[r_uhist])
                            rel(b)
                    if need_utok:
                        b = acq()
                        for kc in range(DC):
                            OP("pe", lambda e, kc=kc, b=b, wsl=wsl: e.matmul(
                                banks[b][:NS, :512], xnT_l[:, kc, N - 16:N], wsl[:, kc, :512],
                                start=(kc == 0), stop=(kc == DC - 1)),
                               reads=[rw, r_xnT_l[kc]], writes=[r_bank[b]])
                        OP("act", lambda e, b=b, p=p: e.copy(utok[:NS, p * 512:(p + 1) * 512], banks[b][:NS, :512]),
                           reads=[r_bank[b]], writes=[r_utok])
                        rel(b)
                    wrel()
                if need_utok:
                    dst = pool_p if kind == "own" else pool_s
                    OP("sp", lambda e, dst=dst: e.dma_start(out=dst[:, :], in_=utok[1:NS, :]),
                       reads=[r_utok], writes=[r_out], dma=True)
                    last_out.append(1)

            ck("A: u proj done")
            if full:
                for p in range(2):
                    wsl, rw = wget(wsrc(w_in, 0, D, PW + 512 * p, PW + 512 * p + 512), DC)
                    for mcl in range(4):
                        hh = 4 * p + mcl
                        b = proj_fm(wsl, rw, mcl, DC, lambda kc: xnT_l[:, kc, :N], r_xnT_l, N)
                        OP("act", lambda e, b=b, hh=hh: e.activation(
                            out=QT[:, hh, :N], in_=banks[b][:, :N], func=AF.Copy, scale=float(HD ** -0.5)),
                           reads=[r_bank[b]], writes=[r_QT[hh]])
                        rel(b)
                    wrel()

            ck("A: q proj done")
            nko, nvo = (nk_own, nv_own) if kind == "own" else (nk_s, nv_s)
            ksi = 0
            for which in range(2):
                dstb, r_dstb = (kb, r_kb) if which == 0 else (vb, r_vb)
                oten = nko if which == 0 else nvo
                for p in range(2):
                    c0 = 2 * PW + which * PW + 512 * p
                    wsl, rw = wget(wsrc(w_in, 0, D, c0, c0 + 512), DC)
                    for tb in range(NB):
                        b = acq()
                        for kc in range(DC):
                            OP("pe", lambda e, kc=kc, b=b, tb=tb, wsl=wsl: e.matmul(
                                banks[b][:pn, :512], xnT_l[:, kc, tb * 128:tb * 128 + pn], wsl[:, kc, :512],
                                start=(kc == 0), stop=(kc == DC - 1)),
                               reads=[rw, r_xnT_l[kc]], writes=[r_bank[b]])
                        if full:
                            ki = ksi % 4
                            ksi += 1
                            OP("act", lambda e, b=b, ki=ki: e.copy(kvs[ki][:pn, :], banks[b][:pn, :512]),
                               reads=[r_bank[b]], writes=[r_kvs[ki]])
                            r0 = (g * TN if kind == "own" else 0) + tb * 128
                            OP("sp", lambda e, ki=ki, oten=oten, r0=r0, p=p: e.dma_start(
                                out=oten[r0:r0 + pn, 512 * p:512 * p + 512], in_=kvs[ki][:pn, :]),
                               reads=[r_kvs[ki]], writes=[r_out], dma=True)
                        if which == 0:
                            OP("dve", lambda e, b=b, tb=tb, p=p: e.tensor_copy(
                                kb[:pn, tb, 512 * p:512 * p + 512], banks[b][:pn, :512]),
                               reads=[r_bank[b]], writes=[r_dstb])
                        else:
                            OP("dve", lambda e, b=b, tb=tb, p=p: e.tensor_copy(
                                vb[:pn, 4 * p:4 * p + 4, tb, :], banks[b][:pn, :512].rearrange("p (h d) -> p h d", h=4)),
                               reads=[r_bank[b]], writes=[r_dstb])
                        rel(b)
                    wrel()
            ck("A: kv proj done")
            for h2 in range(NH // 2):
                b = acq()
                bb = banks[b][:, :].bitcast(BF16).rearrange("p (c n) -> p c n", c=2)
                for kk in range(2):
                    hh = 2 * h2 + kk
                    for tb in range(NB):
                        OP("pe", lambda e, kk=kk, hh=hh, tb=tb, bb=bb: e.transpose(
                            out=bb[:, kk, tb * 128:tb * 128 + pn], in_=kb[:pn, tb, hh * 128:(hh + 1) * 128],
                            identity=ident[:pn, :pn]),
                           reads=[r_kb, r_const], writes=[r_bank[b]])
                OP("dve", lambda e, h2=h2, bb=bb: e.tensor_copy(KTt[:, 2 * h2:2 * h2 + 2, :N], bb[:, :, :N]),
                   reads=[r_bank[b]], writes=[r_KTt])
                rel(b)
            ck("A: KT transposes done")
            if kind == "sample":
                for hh in range(NH):
                    OP("sp", lambda e, hh=hh: e.dma_start(out=kts_scr[:, hh, :], in_=KTt[:, hh, :NS]),
                       reads=[r_KTt], writes=[r_sscr], dma=True)
                for hh in range(NH):
                    OP("sp", lambda e, hh=hh: e.dma_start(out=vs_scr[:, hh * HD:(hh + 1) * HD], in_=vb[:NS, hh, 0, :]),
                       reads=[r_vb], writes=[r_sscr], dma=True)
            else:
                ti = tok0 // TN
                for hh in range(NH):
                    OP("sp", lambda e, hh=hh: e.dma_start(out=kt_scr[:, hh, tok0:tok0 + N], in_=KTt[:, hh, :N]),
                       reads=[r_KTt], writes=[r_ktscr[ti]], dma=True)
                for hh in range(NH):
                    OP("sp", lambda e, hh=hh: e.dma_start(
                        out=v_scr.rearrange("p h j d -> p h (j d)")[:, hh, tok0:tok0 + N],
                        in_=vb[:, hh, :, :].rearrange("p t d -> p (t d)")),
                       reads=[r_vb], writes=[r_vscr[ti]], dma=True)
            ck("prior tile done")
            if not full:
                return

            if kind == "sample":
                OP("sp", lambda e: e.dma_start(out=xs_b[0][:15, :PW], in_=sp_in[:, :]),
                   reads=[], writes=[r_xs[0]], dma=True)
                b = acq()
                bv = banks[b][:, :].rearrange("p (c n) -> p c n", c=8)
                for c in range(8):
                    OP("pe", lambda e, c=c, bv=bv: e.transpose(
                        out=bv[:, c, 0:16], in_=xs_b[0][:16, c * 128:(c + 1) * 128], identity=identf[:16, :16]),
                       reads=[r_xs[0], r_const], writes=[r_bank[b]])
                OP("dve", lambda e, bv=bv: e.tensor_copy(uT[:, :, 1:16], bv[:, :, 0:15]),
                   reads=[r_bank[b]], writes=[r_uT])
                rel(b)

            ck("A done")
            L0 = 16 + N
            if kind == "own":
                OP("dve", lambda e: e.tensor_copy(uT[:, :, 0:16], uhist[:, :, :]), reads=[r_uhist], writes=[r_uT])
            for gi, w in enumerate(WIN):
                c0 = 2 * gi
                cur, rcur = uT[:, c0:c0 + 2, :], r_uT
                lo = 0
                bufs = [(tA, r_tA), (tB, r_tB)]
                for k in range(gi + 1):
                    sh = 1 << k
                    lo += sh
                    dstt, rd = bufs[k % 2]
                    OP("dve", lambda e, cur=cur, dstt=dstt, lo=lo, sh=sh: e.tensor_tensor(
                        dstt[:, :, lo:L0], cur[:, :, lo:L0], cur[:, :, lo - sh:L0 - sh], ALU.add),
                       reads=[rcur], writes=[rd])
                    cur, rcur = dstt, rd
                OP("dve", lambda e, cur=cur, c0=c0, w=w: e.scalar_tensor_tensor(
                    diffT[:, c0:c0 + 2, :N], cur[:, :, 16:16 + N], 1.0 / w, uT[:, c0:c0 + 2, 16:16 + N],
                    ALU.mult, ALU.subtract),
                   reads=[rcur, r_uT], writes=[r_diff[gi]])
                if kind == "own" and g == 0:
                    for cc in range(2):
                        OP("dve", lambda e, cur=cur, cc=cc, gi=gi: e.tensor_tensor(
                            cur[:, cc, 16:32], cur[:, cc, 16:32], cntinv[:, gi, :], ALU.mult),
                           reads=[rcur, r_const], writes=[rcur])
                        OP("dve", lambda e, cur=cur, cc=cc, c0=c0: e.tensor_tensor(
                            diffT[:, c0 + cc, 0:16], cur[:, cc, 16:32], uT[:, c0 + cc, 16:32], ALU.subtract),
                           reads=[rcur, r_uT], writes=[r_diff[gi]])
                for ecl in range(2):
                    b = acq()
                    for cc in range(2):
                        OP("pe", lambda e, b=b, gi=gi, cc=cc, ecl=ecl, c0=c0: e.matmul(
                            banks[b][:, :N], wpool[:, gi, cc, ecl * 128:(ecl + 1) * 128], diffT[:, c0 + cc, :N],
                            start=(cc == 0), stop=(cc == 1)),
                           reads=[r_const, r_diff[gi]], writes=[r_bank[b]])
                    OP("dve", lambda e, b=b, c0=c0, ecl=ecl: e.tensor_scalar(
                        oaT[:, c0 + ecl, :N], banks[b][:, :N], spool[:, c0 + ecl:c0 + ecl + 1], None, ALU.mult),
                       reads=[r_bank[b], r_const], writes=[r_oaT[c0 + ecl]])
                    rel(b)
            if kind == "own" and not last_own:
                OP("dve", lambda e: e.tensor_copy(uhist[:, :, :], uT[:, :, N:N + 16]), reads=[r_uT], writes=[r_uhist])

            ck("B done")
            alias(XC, XA)
            if kind == "own":
                nkb = (tok0 + N) // 128
                kblocks = [(j, 128, (j - (nkb - NB)) if j >= nkb - NB else None) for j in range(nkb)]
            else:
                nkb = past_blks + 1
                kblocks = [(j, 128, None) for j in range(past_blks)] + [(past_blks, NS, 0)]

            def load_ctx(hh):
                hb = hh % 2
                if kind == "own":
                    ntile = (tok0 + N) // TN
                    OP("sp", lambda e: e.dma_start(out=KTh[hb][:, :nkb * 128], in_=kt_scr[:, hh, 0:nkb * 128]),
                       reads=r_ktscr[:ntile], writes=[r_KTh[hb]], dma=True)
                    OP("sp", lambda e: e.dma_start(out=Vh[hb][:, :nkb, :], in_=v_scr[:, hh, 0:nkb, :]),
                       reads=r_vscr[:ntile], writes=[r_Vh[hb]], dma=True)
                else:
                    OP("pool", lambda e: e.dma_start(
                        out=ckst[:, :past_blks, :], in_=ck_in[:, hh * HD:(hh + 1) * HD].rearrange("(j p) d -> p j d", p=128)),
                       reads=[], writes=[r_ckst], dma=True)
                    OP("pool", lambda e: e.dma_start(
                        out=Vh[hb][:, :past_blks, :], in_=cv_in[:, hh * HD:(hh + 1) * HD].rearrange("(j p) d -> p j d", p=128)),
                       reads=[], writes=[r_Vh[hb]], dma=True)
                    OP("sp", lambda e: e.dma_start(out=Vh[hb][:NS, past_blks, :], in_=vs_scr[:, hh * HD:(hh + 1) * HD]),
                       reads=[r_sscr], writes=[r_Vh[hb]], dma=True)
                    OP("sp", lambda e: e.dma_start(out=KTh[hb][:, PAST:PAST + NS], in_=kts_scr[:, hh, :]),
                       reads=[r_sscr], writes=[r_KTh[hb]], dma=True)
                    j0 = 0
                    while j0 < past_blks:
                        nj = min(8, past_blks - j0)
                        b = acq()
                        bb = banks[b][:, :].bitcast(BF16)
                        for jj in range(nj):
                            OP("pe", lambda e, jj=jj, j0=j0, bb=bb: e.transpose(
                                out=bb[:, jj * 128:(jj + 1) * 128], in_=ckst[:, j0 + jj, :], identity=ident[:, :]),
                               reads=[r_ckst, r_const], writes=[r_bank[b]])
                        OP("dve", lambda e, j0=j0, nj=nj, bb=bb: e.tensor_copy(
                            KTh[hb][:, j0 * 128:(j0 + nj) * 128], bb[:, :nj * 128]),
                           reads=[r_bank[b]], writes=[r_KTh[hb]])
                        rel(b)
                        j0 += nj

            NBF = 3
            units = []
            for hh in range(NH):
                rb = list(reversed(kblocks))
                for idx, (j, kp, jj) in enumerate(rb):
                    units.append((hh, j, kp, jj, idx == 0, idx == len(rb) - 1))
            ubank = {}
            obank = {}

            zb1, zb2 = {}, {}

            def uinfo(n):
                hh, j, kp, jj, first, last = units[n]
                return hh, j, kp, jj, first, last, hh % 2, n % NBF

            def P1(n):
                hh, j, kp, jj, first, last, hb, i3 = uinfo(n)
                kslice = KTh[hb][:, j * 128:j * 128 + kp]
                z1 = zb1[n] = acq()
                OP("pe", lambda e: e.matmul(banks[z1][:kp, :N], kslice, QT[:, hh, :N], start=True, stop=True),
                   reads=[r_KTh[hb], r_QT[hh]], writes=[r_bank[z1]])

            def A1(n):
                hh, j, kp, jj, first, last, hb, i3 = uinfo(n)
                z1 = zb1.pop(n)
                OP("act", lambda e: e.activation(out=Eb[i3][:kp, :N], in_=banks[z1][:kp, :N], func=AF.Exp),
                   reads=[r_bank[z1]], writes=[r_E[i3]])
                rel(z1)

            def A2(n):
                hh, j, kp, jj, first, last, hb, i3 = uinfo(n)
                OP("act", lambda e: e.activation(out=Lb[i3][:kp, :N], in_=Eb[i3][:kp, :N], func=AF.Ln, bias=1.0),
                   reads=[r_E[i3]], writes=[r_L[i3]])
                if jj is not None:
                    OP("dve", lambda e: e.tensor_tensor(Lb[i3][:kp, :N], Lb[i3][:kp, :N], masks[:kp, jj, :N], ALU.mult),
                       reads=[r_L[i3], r_const], writes=[r_L[i3]])

            def P2(n):
                hh, j, kp, jj, first, last, hb, i3 = uinfo(n)
                ip = (n - 1) % NBF
                kslice = KTh[hb][:, j * 128:j * 128 + kp]
                z2 = zb2[n] = acq()
                OP("pe", lambda e: e.matmul(banks[z2][:kp, :N], kslice, QT[:, hh, :N], start=True, stop=False),
                   reads=[r_KTh[hb], r_QT[hh]], writes=[r_bank[z2]])
                OP("pe", lambda e: e.matmul(banks[z2][:kp, :N], negtri[:kp, :kp], Lb[i3][:kp, :N], start=False, stop=first),
                   reads=[r_const, r_L[i3]], writes=[r_bank[z2]])
                if not first:
                    OP("pe", lambda e: e.matmul(banks[z2][:kp, :N], negones[:, :kp], Sbb[ip][:, :N], start=False, stop=True),
                       reads=[r_const, r_Sb[ip]], writes=[r_bank[z2]])

            def A3(n):
                hh, j, kp, jj, first, last, hb, i3 = uinfo(n)
                z2 = zb2.pop(n)
                OP("act", lambda e: e.activation(out=Ab[i3][:kp, :N], in_=banks[z2][:kp, :N], func=AF.Exp),
                   reads=[r_bank[z2]], writes=[r_A[i3]])
                rel(z2)
                if jj is not None:
                    OP("dve", lambda e: e.tensor_tensor(Ab[i3][:kp, :N], Ab[i3][:kp, :N], masks[:kp, jj, :N], ALU.mult),
                       reads=[r_A[i3], r_const], writes=[r_A[i3]])

            def D1(n):
                hh, j, kp, jj, first, last, hb, i3 = uinfo(n)
                if first:
                    OP("dve", lambda e: e.memset(S32[:, :N], 0.0), reads=[], writes=[r_S32])
                if not last:
                    OP("dve", lambda e: e.tensor_tensor(S32[:kp, :N], S32[:kp, :N], Lb[i3][:kp, :N], ALU.add),
                       reads=[r_S32, r_L[i3]], writes=[r_S32])
                    OP("dve", lambda e: e.tensor_copy(Sbb[i3][:, :N], S32[:, :N]),
                       reads=[r_S32], writes=[r_Sb[i3]])

            def P3(n):
                hh, j, kp, jj, first, last, hb, i3 = uinfo(n)
                if first:
                    obank[hh] = acq()
                bo = obank[hh]
                OP("pe", lambda e: e.matmul(banks[bo][:, :N], Vh[hb][:kp, j, :], Ab[i3][:kp, :N], start=first, stop=last),
                   reads=[r_Vh[hb], r_A[i3]], writes=[r_bank[bo]])
                if last:
                    OP("dve", lambda e: e.tensor_copy(obT[:, hh, :N], banks[bo][:, :N]),
                       reads=[r_bank[bo]], writes=[r_obT[hh]])
                    rel(bo)
                    if hh + 2 < NH:
                        load_ctx(hh + 2)

            load_ctx(0)
            load_ctx(1)
            nu = len(units)
            assert nu // NH >= 4

            def run(fn, n):
                if 0 <= n < nu:
                    fn(n)

            for t in range(nu + 4):
                run(P1, t)
                run(A1, t - 1)
                run(P2, t - 2)
                run(A3, t - 3)
                run(A2, t - 1)
                run(D1, t - 2)
                run(P3, t - 4)

            ck("C done")
            alias(XD, XC + XA)
            for tb in range(NB):
                OP("sp", lambda e, tb=tb: e.dma_start(out=ht[:pn, tb, :], in_=x_rows[tb * 128:tb * 128 + pn, :]),
                   reads=[], writes=[r_ht[tb]], dma=True)
            for p in range(4):
                wb, rwb = wget(wsrc(w_branch, 0, D, 512 * p, 512 * p + 512), DC)
                wga, rwga = wget(wsrc(w_in, 0, D, 4 * PW + 512 * p, 4 * PW + 512 * p + 512), DC)
                bybs = []
                for mcl in range(4):
                    bya = proj_fm(wb, rwb, mcl, 8, lambda kc: oaT[:, kc, :N], r_oaT, N)
                    bga = proj_fm(wga, rwga, mcl, DC, lambda kc: xnT[:, kc, :N], r_xnT, N)
                    byb = proj_fm(wb, rwb, mcl, 8, lambda kc: obT[:, kc, :N], r_obT, N, kofs=8)
                    bybs.append(byb)
                    OP("act", lambda e, bga=bga: e.activation(out=sAb[0][:, :N], in_=banks[bga][:, :N], func=AF.Sigmoid),
                       reads=[r_bank[bga]], writes=[r_sA[0]])
                    OP("dve", lambda e, bya=bya, mcl=mcl: e.tensor_tensor(t1b[mcl][:, :N], banks[bya][:, :N], sAb[0][:, :N], ALU.mult),
                       reads=[r_bank[bya], r_sA[0]], writes=[r_t1[mcl]])
                    rel(bya)
                    rel(bga)
                wrel()
                wrel()
                wgb, rwgb = wget(wsrc(w_in, 0, D, 4 * PW + D + 512 * p, 4 * PW + D + 512 * p + 512), DC)
                for mcl in range(4):
                    mc = 4 * p + mcl
                    byb = bybs[mcl]
                    bgb = proj_fm(wgb, rwgb, mcl, DC, lambda kc: xnT[:, kc, :N], r_xnT, N)
                    OP("act", lambda e, bgb=bgb: e.activation(out=sBb[0][:, :N], in_=banks[bgb][:, :N], func=AF.Sigmoid),
                       reads=[r_bank[bgb]], writes=[r_sB[0]])
                    OP("dve", lambda e, byb=byb: e.tensor_tensor(sBb[0][:, :N], banks[byb][:, :N], sBb[0][:, :N], ALU.mult),
                       reads=[r_bank[byb], r_sB[0]], writes=[r_sB[0]])
                    OP("dve", lambda e, mc=mc, mcl=mcl: e.tensor_tensor(mergedT[:, mc, :N], t1b[mcl][:, :N], sBb[0][:, :N], ALU.add),
                       reads=[r_t1[mcl], r_sB[0]], writes=[r_mer[mc]])
                    rel(bgb)
                    rel(byb)
                wrel()

            ck("D done")
            for cg in range(4):
                wsl, rw = wget(wsrc(w_out, 0, D, 512 * cg, 512 * cg + 512), DC)
                for tb in range(NB):
                    b = acq()
                    for kc in range(DC):
                        OP("pe", lambda e, kc=kc, b=b, tb=tb, wsl=wsl: e.matmul(
                            banks[b][:pn, :512], mergedT[:, kc, tb * 128:tb * 128 + pn], wsl[:, kc, :512],
                            start=(kc == 0), stop=(kc == DC - 1)),
                           reads=[rw, r_mer[kc]], writes=[r_bank[b]])
                    OP("dve", lambda e, b=b, tb=tb, cg=cg: e.tensor_tensor(
                        ht[:pn, tb, 512 * cg:512 * cg + 512], banks[b][:pn, :512], ht[:pn, tb, 512 * cg:512 * cg + 512], ALU.add),
                       reads=[r_bank[b], r_ht[tb]], writes=[r_ht[tb]])
                    rel(b)
                wrel()

            ck("E done")
            alias(r_n2t, r_mer)
            for tb in range(NB):
                rmsnorm_rows(ht[:pn, tb, :], r_ht[tb], pn, n2t[:pn, tb, :], r_n2t[tb], tb)
            transposes_to_T(n2t, r_n2t, NB, pn, N, gffn)

            ck("F done")
            alias(r_hid, r_n2t + r_mer)
            gi_ = 0
            for pp in range(FC // 4):
                wg, rwg = wget(wsrc(w_gu, 0, D, 512 * pp, 512 * pp + 512), DC)
                wu, rwu = wget(wsrc(w_gu, 0, D, DFF + 512 * pp, DFF + 512 * pp + 512), DC)
                for fcl in range(4):
                    fc = 4 * pp + fcl
                    i2 = gi_ % 2
                    gi_ += 1
                    bg = proj_fm(wg, rwg, fcl, DC, lambda kc: xnT[:, kc, :N], r_xnT, N)
                    bu = proj_fm(wu, rwu, fcl, DC, lambda kc: xnT[:, kc, :N], r_xnT, N)
                    OP("act", lambda e, bg=bg, i2=i2: e.activation(out=t1b[i2][:, :N], in_=banks[bg][:, :N], func=AF.Silu),
                       reads=[r_bank[bg]], writes=[r_t1[i2]])
                    OP("dve", lambda e, bu=bu, i2=i2, fc=fc: e.tensor_tensor(hidT[:, fc, :N], banks[bu][:, :N], t1b[i2][:, :N], ALU.mult),
                       reads=[r_bank[bu], r_t1[i2]], writes=[r_hid[fc]])
                    rel(bg)
                    rel(bu)
                wrel()
                wrel()

            ck("G done")
            kparts = [(0, 16), (16, 32), (32, FC)]
            for cg in range(4):
                bs = [acq() for _ in range(NB)]
                for (k0, k1) in kparts:
                    wsl, rw = wget(wsrc(w_down, k0 * 128, k1 * 128, 512 * cg, 512 * cg + 512), k1 - k0)
                    for tb in range(NB):
                        for kc in range(k0, k1):
                            OP("pe", lambda e, kc=kc, k0=k0, tb=tb, wsl=wsl, b=bs[tb]: e.matmul(
                                banks[b][:pn, :512], hidT[:, kc, tb * 128:tb * 128 + pn], wsl[:, kc - k0, :512],
                                start=(kc == 0), stop=(kc == FC - 1)),
                               reads=[rw, r_hid[kc]], writes=[r_bank[bs[tb]]])
                    wrel()
                for tb in range(NB):
                    OP("dve", lambda e, b=bs[tb], tb=tb, cg=cg: e.tensor_tensor(
                        ht[:pn, tb, 512 * cg:512 * cg + 512], banks[b][:pn, :512], ht[:pn, tb, 512 * cg:512 * cg + 512], ALU.add),
                       reads=[r_bank[bs[tb]], r_ht[tb]], writes=[r_ht[tb]])
                    rel(bs[tb])

            ck("H done")
            alias(r_n2t, r_hid)
            yout = y_own if kind == "own" else y_s
            for tb in range(NB):
                rmsnorm_rows(ht[:pn, tb, :], r_ht[tb], pn, ht[:pn, tb, :], r_ht[tb], tb, gscale=gfin[:pn, :])
                r0 = (g * TN if kind == "own" else 0) + tb * 128
                OP("sp", lambda e, tb=tb, r0=r0: e.dma_start(out=yout[r0:r0 + pn, :], in_=ht[:pn, tb, :]),
                   reads=[r_ht[tb]], writes=[r_out], dma=True)

        try:
            ck("consts")
            for g in range(n_prior_tiles):
                tile("prior", g, xp_pri[g * TN:(g + 1) * TN, :], TN, g * TN)
            for g in range(n_own_tiles):
                tile("own", g, xp_own[g * TN:(g + 1) * TN, :], TN, NPRI + g * TN)
                ck("own tile done")
            tile("sample", 0, xs_in, NS, 0)
        except _Stop:
            pass
        return wreq

    for r in R.values():
        r.last_w, r.readers = None, []
    wpid, wconv, wscr, r_wscr = {}, {}, None, []
    wlist = flow(Prog(dry=True), None)
    for key, src, kcn in wlist:
        if key not in wpid:
            wpid[key] = len(wpid)
            wconv[key] = (src, kcn)
    wscr = nc.dram_tensor("wscr", [len(wpid), 128, 16, 512], BF16, kind="Internal").ap()
    r_wscr = [Res(f"wscr{i}") for i in range(len(wpid))]
    prog = Prog()
    flow(prog, wlist)

    for e in ("pe", "act", "dve", "pool"):
        n = 0
        for o in prog.streams[e]:
            if o.dma:
                continue
            if o.signaled:
                o.sig = (sems_eng[e][(n // EPOCH) % 4], n % EPOCH + 1)
                n += 1
        assert n < 4 * EPOCH, (e, n)
    nd = 0
    dma_prev = {}
    for e in ("pool", "sp"):
        pass
    allops = []
    for e in Prog.ENGS:
        for o in prog.streams[e]:
            if o.dma:
                allops.append(o)
    qsems = {"pool": sems_dma[:NDL // 2], "sp": sems_dma[NDL // 2:]}
    for e in ("pool", "sp"):
        k = 0
        ss = qsems[e]
        for o in prog.streams[e]:
            if not o.dma:
                continue
            s = ss[k % len(ss)]
            val = 16 * (k // len(ss) + 1)
            o.sig = (s, val)
            o.idx = (s, val - 16)
            k += 1

    def emit_stream(ename, eng):
        waited = {}
        for o in prog.streams[ename]:
            need = {}
            if o.dma and o.idx[1] > 0:
                need[id(o.idx[0])] = (o.idx[0], o.idx[1])
            for d in o.deps:
                s, v = d.sig
                k = id(s)
                if k not in need or need[k][1] < v:
                    need[k] = (s, v)
            for k, (s, v) in need.items():
                if waited.get(k, 0) >= v:
                    continue
                eng.wait_ge(s, v)
                waited[k] = v
            ins = o.fn(eng)
            if o.dma:
                ins.then_inc(o.sig[0], 16)
            elif o.signaled:
                ins.then_inc(o.sig[0], 1)
        if ename in ("sp", "pool"):
            last = {}
            for o in prog.streams[ename]:
                if o.dma:
                    last[id(o.sig[0])] = o.sig
            for s, v in last.values():
                eng.wait_ge(s, v)

    with nc.Block() as block:
        @block.tensor
        def _(e):
            emit_stream("pe", e)

        @block.scalar
        def _(e):
            emit_stream("act", e)

        @block.vector
        def _(e):
            emit_stream("dve", e)

        @block.gpsimd
        def _(e):
            emit_stream("pool", e)

        @block.sync
        def _(e):
            emit_stream("sp", e)
    es.close()
    return nc


def _consts(h_is_first):
    ident = np.eye(128, dtype=np.float32)
    s_ = np.arange(128)
    negtri = -(s_[:, None] >= s_[None, :]).astype(np.float32)
    negones = -np.ones((128, 128), np.float32)
    t_ = np.arange(TN)
    masks = np.stack([((128 * jj + s_)[:, None] < t_[None, :]).astype(np.float32) for jj in range(4)], axis=1)
    cnt = np.empty((4, 16), np.float32)
    for gi, w in enumerate(WIN):
        if h_is_first:
            cnt[gi] = 1.0 / np.minimum(np.arange(16) + 1, w)
        else:
            cnt[gi] = 1.0 / w
    cnt = np.ascontiguousarray(np.broadcast_to(cnt.reshape(1, 64), (128, 64)))
    return ident, negtri, negones, np.ascontiguousarray(masks.reshape(128, 4 * TN)), cnt


def _run(inputs, n_own_tiles, n_prior_tiles, past_blks, batch, dec_batch):
    f = lambda a: np.ascontiguousarray(np.asarray(a, dtype=np.float32))
    x_prompt, x_sample = f(inputs["x_prompt"]), f(inputs["x_sample"])
    cache_k, cache_v, state_pool = f(inputs["cache_k"]), f(inputs["cache_v"]), f(inputs["state_pool"])
    NOWN, NPRI = n_own_tiles * TN, n_prior_tiles * TN
    seq = x_prompt.shape[1]
    assert seq == NOWN + NPRI and NOWN == NPRI
    past = past_blks * 128
    nc = build(n_own_tiles, n_prior_tiles, past_blks)

    def vec16(v):
        return np.ascontiguousarray(f(v).reshape(-1, 128).T)

    shared = {
        "w_in": f(inputs["w_in"][0]), "w_pool": f(inputs["w_pool"][0]), "w_branch": f(inputs["w_branch"][0]),
        "w_out": f(inputs["w_out"][0]), "w_gu": f(inputs["w_gate_up"][0]), "w_down": f(inputs["w_down"][0]),
        "gmix": vec16(inputs["g_mix"][0]), "gffn": vec16(inputs["g_ffn"][0]), "spool": vec16(inputs["s_pool"][0]),
        "gfin": np.ascontiguousarray(np.broadcast_to(f(inputs["g_final"]).reshape(1, D), (128, D))),
    }
    in_maps = []
    ncores = 2 * batch
    assert ncores == dec_batch
    for c in range(ncores):
        b, h = c // 2, c % 2
        ident, negtri, negones, masks, cnt = _consts(h == 0)
        m = dict(shared)
        m["xp_own"] = np.ascontiguousarray(x_prompt[b, h * NOWN:(h + 1) * NOWN])
        m["xp_pri"] = np.ascontiguousarray(x_prompt[b, :NPRI]) if h == 1 else np.zeros((NPRI, D), np.float32)
        m["xs_in"] = np.ascontiguousarray(x_sample[c])
        m["ck_in"] = np.ascontiguousarray(cache_k[0, c].reshape(past, NH * HD))
        m["cv_in"] = np.ascontiguousarray(cache_v[0, c].reshape(past, NH * HD))
        m["sp_in"] = np.ascontiguousarray(state_pool[0, c])
        m.update(ident=ident, negtri=negtri, negones=negones, masks=masks, cntinv=cnt)
        in_maps.append(m)
    res = run_bass_kernel_spmd(nc, in_maps, core_ids=list(range(ncores)))
    rs = res.results
    y_prompt = np.empty((batch, seq, D), np.float32)
    nkp = np.empty((1, batch, seq, NH, HD), np.float32)
    nvp = np.empty((1, batch, seq, NH, HD), np.float32)
    npp = np.empty((1, batch, 15, PW), np.float32)
    y_sample = np.empty((dec_batch, NS, D), np.float32)
    nks = np.empty((1, dec_batch, NS, NH, HD), np.float32)
    nvs = np.empty((1, dec_batch, NS, NH, HD), np.float32)
    nps = np.empty((1, dec_batch, 15, PW), np.float32)
    for c in range(ncores):
        b, h = c // 2, c % 2
        r = rs[c]
        y_prompt[b, h * NOWN:(h + 1) * NOWN] = r["y_own"]
        nkp[0, b, h * NOWN:(h + 1) * NOWN] = r["nk_own"].reshape(NOWN, NH, HD)
        nvp[0, b, h * NOWN:(h + 1) * NOWN] = r["nv_own"].reshape(NOWN, NH, HD)
        if h == 1:
            npp[0, b] = r["pool_p"]
        y_sample[c] = r["y_s"]
        nks[0, c] = r["nk_s"].reshape(NS, NH, HD)
        nvs[0, c] = r["nv_s"].reshape(NS, NH, HD)
        nps[0, c] = r["pool_s"]
    return (y_prompt, y_sample, nkp, nvp, npp, nks, nvs, nps)


def kernel(**inputs):
    return _run(inputs, 4, 4, 32, 4, 8)
```

```python
import os
import numpy as np
import concourse.bass as bass
import concourse.mybir as mybir
from concourse.bass_utils import run_bass_kernel_spmd

F32 = mybir.dt.float32
BF16 = mybir.dt.bfloat16
AF = mybir.ActivationFunctionType
ALU = mybir.AluOpType

D = 2048
DC = 16
NH = 8
HD = 128
PW = 1024
DFF = 5632
FC = 44
INW = 8192
NS = 16
EPS = 1e-6
TN = 512
WIN = (2, 4, 8, 16)
NCORES = 8
WRING = 3
EPOCH = 24000


class _Stop(Exception):
    pass


class Res:
    __slots__ = ("last_w", "readers", "name")

    def __init__(self, name=""):
        self.last_w = None
        self.readers = []
        self.name = name


class Op:
    __slots__ = ("eng", "fn", "deps", "signaled", "sig", "dma", "idx")

    def __init__(self, eng, fn, dma):
        self.eng = eng
        self.fn = fn
        self.deps = ()
        self.signaled = False
        self.sig = None
        self.dma = dma


class Prog:
    ENGS = ("pe", "act", "dve", "pool", "sp")

    def __init__(self, dry=False):
        self.dry = dry
        self.streams = {e: [] for e in self.ENGS}

    def op(self, eng, fn, reads=(), writes=(), dma=False):
        if self.dry:
            return None
        o = Op(eng, fn, dma)
        deps = {}
        for r in reads:
            w = r.last_w
            if w is not None:
                deps[id(w)] = w
        for wr in writes:
            w = wr.last_w
            if w is not None:
                deps[id(w)] = w
            for rd in wr.readers:
                deps[id(rd)] = rd
        if eng == "pe" and not dma:
            deps = {k: v for k, v in deps.items() if v.dma or v.eng != "pe"}
        o.deps = tuple(deps.values())
        for d in o.deps:
            d.signaled = True
        for r in reads:
            r.readers.append(o)
        for wr in writes:
            wr.last_w = o
            wr.readers = []
        self.streams[eng].append(o)
        return o


def alias(new, old):
    pend = []
    for r in old:
        if r.last_w is not None:
            pend.append(r.last_w)
        pend.extend(r.readers)
    if not pend:
        return
    for r in new:
        r.readers = r.readers + pend


def build(n_own_tiles, n_prior_tiles, past_blks):
    NOWN = n_own_tiles * TN
    NPRI = n_prior_tiles * TN
    SEQL = NOWN + NPRI
    NBLK = SEQL // 128
    PAST = past_blks * 128
    MAXKB = max(NBLK, past_blks + 1)

    nc = bass.Bass("TRN2", target_bir_lowering=False)

    def din(name, shape, dt=F32):
        return nc.dram_tensor(name, list(shape), dt, kind="ExternalInput").ap()

    def dout(name, shape, dt=F32):
        return nc.dram_tensor(name, list(shape), dt, kind="ExternalOutput").ap()

    xp_own = din("xp_own", [NOWN, D])
    xp_pri = din("xp_pri", [NPRI, D])
    xs_in = din("xs_in", [NS, D])
    ck_in = din("ck_in", [PAST, NH * HD])
    cv_in = din("cv_in", [PAST, NH * HD])
    sp_in = din("sp_in", [15, PW])
    w_in = din("w_in", [D, INW])
    w_pool = din("w_pool", [4, 256, 256])
    w_branch = din("w_branch", [D, D])
    w_out = din("w_out", [D, D])
    w_gu = din("w_gu", [D, 2 * DFF])
    w_down = din("w_down", [DFF, D])
    gmix_in = din("gmix", [128, DC])
    gffn_in = din("gffn", [128, DC])
    spool_in = din("spool", [128, 8])
    gfin_in = din("gfin", [128, D])
    cnt_in = din("cntinv", [128, 4 * 16])
    ident_in = din("ident", [128, 128])
    negtri_in = din("negtri", [128, 128])
    negones_in = din("negones", [128, 128])
    masks_in = din("masks", [128, 4 * TN])

    y_own = dout("y_own", [NOWN, D])
    nk_own = dout("nk_own", [NOWN, NH * HD])
    nv_own = dout("nv_own", [NOWN, NH * HD])
    pool_p = dout("pool_p", [15, PW])
    y_s = dout("y_s", [NS, D])
    nk_s = dout("nk_s", [NS, NH * HD])
    nv_s = dout("nv_s", [NS, NH * HD])
    pool_s = dout("pool_s", [15, PW])

    kt_scr = nc.dram_tensor("kt_scr", [128, NH, SEQL], BF16, kind="Internal").ap()
    v_scr = nc.dram_tensor("v_scr", [128, NH, NBLK, HD], BF16, kind="Internal").ap()
    kts_scr = nc.dram_tensor("kts_scr", [128, NH, NS], BF16, kind="Internal").ap()
    vs_scr = nc.dram_tensor("vs_scr", [NS, NH * HD], BF16, kind="Internal").ap()

    XBYTES = 99072
    from contextlib import ExitStack
    es = ExitStack()

    def sb(name, shape, dt):
        return es.enter_context(nc.sbuf_tensor(name, list(shape), dt))

    X = sb("X", [128, XBYTES // 4], F32)
    wring = [sb(f"wr{i}", [128, 16, 512], BF16) for i in range(WRING)]
    xnT = sb("xnT", [128, DC, TN], BF16)
    QT = sb("QT", [128, NH, TN], BF16)
    oaT = sb("oaT", [128, 8, TN], BF16)
    obT = sb("obT", [128, NH, TN], BF16)
    ident = sb("identb", [128, 128], BF16)
    negtri = sb("negtrib", [128, 128], BF16)
    negones = sb("negonesb", [128, 128], BF16)
    masks = sb("masksb", [128, 4, TN], BF16)
    identf = sb("identf", [128, 128], F32)
    gmix = sb("gmixs", [128, DC], F32)
    gffn = sb("gffns", [128, DC], F32)
    spool = sb("spools", [128, 8], F32)
    gfin = sb("gfins", [128, D], F32)
    cntinv = sb("cntinvs", [128, 4, 16], F32)
    wpool = sb("wpools", [128, 4, 2, 256], BF16)
    stats = sb("stats", [128, 16], F32)
    utok = sb("utok", [NS, PW], F32)
    uhist = sb("uhist", [128, 8, 16], F32)

    def xv(off, nbytes, dt, pat=None, **kw):
        a = X[:, off // 4:(off + nbytes) // 4]
        if dt == BF16:
            a = a.bitcast(BF16)
        if pat:
            a = a.rearrange(pat, **kw)
        return a

    xs_b = [xv(0, 8192, F32), xv(8192, 8192, F32)]
    xnt = xv(16384, 16384, BF16, "p (t f) -> p t f", t=4)
    UW = 16 + TN
    uT = xv(32768, 8 * UW * 4, F32, "p (c n) -> p c n", c=8)
    o = 32768 + 8 * UW * 4
    tA = xv(o, 2 * UW * 4, F32, "p (c n) -> p c n", c=2)
    o += 2 * UW * 4
    tB = xv(o, 2 * UW * 4, F32, "p (c n) -> p c n", c=2)
    o += 2 * UW * 4
    diffT = xv(o, 8192, BF16, "p (c n) -> p c n", c=8)
    o += 8192
    kvs = [xv(o + i * 2048, 2048, F32) for i in range(4)]
    o += 8192
    kb = xv(o, 8192, BF16, "p (t f) -> p t f", t=4)
    o += 8192
    vb = xv(o, 8192, BF16, "p (h t d) -> p h t d", h=NH, t=4)
    o += 8192
    KTt = xv(o, 8192, BF16, "p (c n) -> p c n", c=8)
    o += 8192
    assert o <= XBYTES, o
    xnt_alt = xv(32768, 16384, BF16, "p (t f) -> p t f", t=4)
    xnT_alt = xv(49152, 16384, BF16, "p (c n) -> p c n", c=DC)
    KB2 = MAXKB * 128 * 2
    assert KB2 <= 9216
    KTh = [xv(0, 9216, BF16), xv(9216, 9216, BF16)]
    Vh = [xv(18432, 9216, BF16, "p (j d) -> p j d", d=HD), xv(27648, 9216, BF16, "p (j d) -> p j d", d=HD)]
    Eb = [xv(36864 + 2048 * i, 2048, F32) for i in range(3)]
    Lb = [xv(43008 + 1024 * i, 1024, BF16) for i in range(3)]
    Ab = [xv(46080 + 1024 * i, 1024, BF16) for i in range(3)]
    S32 = xv(49152, 2048, F32)
    Sbb = [xv(51200 + 1024 * i, 1024, BF16) for i in range(3)]
    ckst = xv(54272, 8192, BF16, "p (j d) -> p j d", d=HD)
    ht = xv(0, 32768, F32, "p (t f) -> p t f", t=4)
    hidT = xv(32768, 45056, BF16, "p (c n) -> p c n", c=FC)
    mergedT = xv(32768, 16384, BF16, "p (c n) -> p c n", c=DC)
    n2t = xv(32768, 16384, BF16, "p (t f) -> p t f", t=4)
    t1b = [xv(77824 + 2048 * i, 2048, F32) for i in range(4)]
    sAb = [xv(86016, 2048, F32)]
    sBb = [xv(88064, 2048, F32)]

    banks = [es.enter_context(nc.psum_tensor(f"bank{i}", [128, 512], F32)) for i in range(8)]

    NDL = 20
    sems_eng = {e: [es.enter_context(nc.semaphore(f"s_{e}{k}")) for k in range(4)] for e in ("pe", "act", "dve", "pool")}
    sems_dma = [es.enter_context(nc.semaphore(f"s_dma{k}")) for k in range(NDL)]

    R = {}

    def res(name):
        if name not in R:
            R[name] = Res(name)
        return R[name]

    r_xs = [res("xs0"), res("xs1")]
    r_xnt = [res(f"xnt{t}") for t in range(4)]
    r_xnT = [res(f"xnT{k}") for k in range(DC)]
    r_uT = res("uT")
    r_xnt_alt = [res(f"xntalt{t}") for t in range(4)]
    r_xnT_alt = [res(f"xnTalt{k}") for k in range(DC)]
    r_tA, r_tB = res("tA"), res("tB")
    r_diff = [res(f"diff{g}") for g in range(4)]
    r_kvs = [res(f"kvs{i}") for i in range(4)]
    r_kb, r_vb, r_KTt = res("kb"), res("vb"), res("KTt")
    r_QT = [res(f"QT{h}") for h in range(NH)]
    r_oaT = [res(f"oaT{c}") for c in range(8)]
    r_obT = [res(f"obT{h}") for h in range(NH)]
    r_KTh = [res("KTh0"), res("KTh1")]
    r_Vh = [res("Vh0"), res("Vh1")]
    r_E = [res(f"E{i}") for i in range(3)]
    r_L = [res(f"L{i}") for i in range(3)]
    r_A = [res(f"A{i}") for i in range(3)]
    r_S32 = res("S32")
    r_Sb = [res(f"Sb{i}") for i in range(3)]
    r_ckst = res("ckst")
    r_ht = [res(f"ht{t}") for t in range(4)]
    r_hid = [res(f"hid{c}") for c in range(FC)]
    r_mer = [res(f"mer{c}") for c in range(DC)]
    r_n2t = [res(f"n2t{t}") for t in range(4)]
    r_sA = [res("sA0")]
    r_sB = [res("sB0")]
    r_t1 = [res(f"t1{i}") for i in range(4)]
    r_bank = [res(f"bank{i}") for i in range(8)]
    r_wr = [res(f"wr{i}") for i in range(WRING)]
    r_const = res("const")
    r_stats = [res(f"st{i}") for i in range(16)]
    r_utok = res("utok")
    r_uhist = res("uhist")
    r_ktscr = [res(f"ktscr{t}") for t in range(SEQL // TN)]
    r_vscr = [res(f"vscr{t}") for t in range(SEQL // TN)]
    r_sscr = res("sscr")
    r_out = res("out")

    XA = r_xs + r_xnt + [r_uT, r_tA, r_tB] + r_diff + r_kvs + [r_kb, r_vb, r_KTt]
    XALT = r_xnt_alt + r_xnT_alt
    XC = r_KTh + r_Vh + r_E + r_L + r_A + [r_S32] + r_Sb + [r_ckst]
    XD = r_ht + r_hid + r_mer + r_n2t + r_sA + r_sB + r_t1

    def flow(prog, wlist):
        wreq = []
        wstate = {"next_load": 0, "released": 0}
        last_out = []

        def OP(eng, fn, reads=(), writes=(), dma=False):
            xb = [r for r in reads if r.name.startswith("bank")]
            if xb:
                writes = list(writes) + xb
            return prog.op(eng, fn, reads, writes, dma)

        ckstate = {"n": 0}
        kstop = int(os.environ.get("KSTOP", "100000"))

        def ck(label=""):
            ckstate["n"] += 1
            if ckstate["n"] == kstop:
                print("KSTOP at", ckstate["n"], label, flush=True)
                raise _Stop()

        wseen = set()

        def wload(i):
            key, src, kcn = wlist[i]
            slot = i % WRING
            pid = wpid[key]
            if key not in wseen:
                wseen.add(key)
                OP("pool", lambda e, s=slot, src=src, k=kcn: e.dma_start(out=wring[s][:, 0:k, :], in_=src),
                   reads=[], writes=[r_wr[slot]], dma=True)
                OP("sp", lambda e, s=slot, pid=pid, k=kcn: e.dma_start(out=wscr[pid, :, 0:k, :], in_=wring[s][:, 0:k, :]),
                   reads=[r_wr[slot]], writes=[r_wscr[pid]], dma=True)
            else:
                OP("sp", lambda e, s=slot, pid=pid, k=kcn: e.dma_start(out=wring[s][:, 0:k, :], in_=wscr[pid, :, 0:k, :]),
                   reads=[r_wscr[pid]], writes=[r_wr[slot]], dma=True)

        def wfill():
            if prog.dry:
                return
            while wstate["next_load"] < min(len(wlist), wstate["released"] + WRING):
                wload(wstate["next_load"])
                wstate["next_load"] += 1

        def wget(srckey, kcn):
            i = len(wreq)
            wreq.append((srckey[0], srckey[1], kcn))
            assert i - wstate["released"] < WRING
            wfill()
            return wring[i % WRING], r_wr[i % WRING]

        def wrel():
            wstate["released"] += 1
            wfill()

        def wsrc(w, r0, r1, c0, c1):
            return ((w.tensor.name, r0, r1, c0, c1), w[r0:r1, c0:c1].rearrange("(kc p) m -> p kc m", p=128))

        free = list(range(8))

        def acq():
            return free.pop(0)

        def rel(b):
            free.append(b)

        def cload(dst, src, eng="sp"):
            OP(eng, lambda e, d=dst, s=src: e.dma_start(out=d, in_=s), reads=[], writes=[r_const], dma=True)

        cload(ident[:], ident_in, "pool")
        cload(negtri[:], negtri_in, "pool")
        cload(negones[:], negones_in, "pool")
        cload(masks[:], masks_in.rearrange("p (j t) -> p j t", j=4), "pool")
        cload(wpool[:], w_pool.rearrange("g (cc p) e -> p g cc e", p=128), "pool")
        cload(identf[:], ident_in)
        cload(gmix[:], gmix_in)
        cload(gffn[:], gffn_in)
        cload(spool[:], spool_in)
        cload(gfin[:], gfin_in)
        cload(cntinv[:], cnt_in.rearrange("p (g t) -> p g t", g=4))
        OP("dve", lambda e: e.memset(uT[:], 0.0), reads=[], writes=[r_uT])
        def rmsnorm_rows(src_ap, r_src, pn, dst_ap, r_dst, si, gscale=None):
            rs = [r_stats[si * 3 + 0], r_stats[si * 3 + 1], r_stats[si * 3 + 2]]
            c0 = si * 3
            ss, rt, rstd = stats[:pn, c0:c0 + 1], stats[:pn, c0 + 1:c0 + 2], stats[:pn, c0 + 2:c0 + 3]
            junk = dst_ap if gscale is None else None
            OP("act", lambda e: e.activation(out=junk if junk is not None else n2t[:pn, si, :], in_=src_ap, func=AF.Square, accum_out=ss),
               reads=[r_src], writes=[rs[0], r_dst if gscale is None else r_n2t[si]])
            OP("act", lambda e: e.activation(out=rt, in_=ss, func=AF.Sqrt, scale=1.0 / D, bias=EPS),
               reads=[rs[0]], writes=[rs[1]])
            OP("dve", lambda e: e.reciprocal(rstd, rt), reads=[rs[1]], writes=[rs[2]])
            if gscale is None:
                OP("dve", lambda e: e.tensor_scalar(dst_ap, src_ap, rstd, None, ALU.mult),
                   reads=[r_src, rs[2]], writes=[r_dst])
            else:
                OP("dve", lambda e: e.scalar_tensor_tensor(dst_ap, src_ap, rstd, gscale, ALU.mult, ALU.mult),
                   reads=[r_src, rs[2], r_const], writes=[r_dst])

        def transposes_to_T(src_t, r_src_t, NB, pn, N, gvec, xnT=xnT, r_xnT=r_xnT):
            for kc2 in range(DC // 2):
                b = acq()
                bb = banks[b][:, :].bitcast(BF16).rearrange("p (c n) -> p c n", c=2)
                for kk in range(2):
                    kc = kc2 * 2 + kk
                    for tb in range(NB):
                        OP("pe", lambda e, kk=kk, kc=kc, tb=tb, bb=bb: e.transpose(
                            out=bb[:, kk, tb * 128:tb * 128 + pn], in_=src_t[:pn, tb, kc * 128:(kc + 1) * 128],
                            identity=ident[:pn, :pn]),
                           reads=[r_src_t[tb], r_const], writes=[r_bank[b]])
                for kk in range(2):
                    kc = kc2 * 2 + kk
                    eng = "dve"
                    if eng == "act":
                        OP("act", lambda e, kk=kk, kc=kc, bb=bb: e.activation(
                            out=xnT[:, kc, :N], in_=bb[:, kk, :N], func=AF.Copy, scale=gvec[:, kc:kc + 1]),
                           reads=[r_bank[b], r_const], writes=[r_xnT[kc]])
                    else:
                        OP("dve", lambda e, kk=kk, kc=kc, bb=bb: e.tensor_scalar(
                            xnT[:, kc, :N], bb[:, kk, :N], gvec[:, kc:kc + 1], None, ALU.mult),
                           reads=[r_bank[b], r_const], writes=[r_xnT[kc]])
                rel(b)

        def proj_fm(wsl, r_w, mcl, kcn, rhs_fn, r_rhs, N, kofs=0):
            b = acq()
            for kc in range(kcn):
                OP("pe", lambda e, kc=kc, b=b: e.matmul(
                    banks[b][:, :N], wsl[:, kofs + kc, mcl * 128:(mcl + 1) * 128], rhs_fn(kc),
                    start=(kc == 0), stop=(kc == kcn - 1)),
                   reads=[r_w, r_rhs[kc]], writes=[r_bank[b]])
            return b

        def tile(kind, g, x_rows, N, tok0):
            NB = (N + 127) // 128
            pn = min(N, 128)
            last_prior = (kind == "prior" and g == n_prior_tiles - 1)
            last_own = (kind == "own" and g == n_own_tiles - 1)
            need_utok = last_own or kind == "sample"
            full = kind != "prior"
            use_alt = (kind == "prior" and g % 2 == 1)
            xnt_l, r_xnt_l = (xnt_alt, r_xnt_alt) if use_alt else (xnt, r_xnt)
            xnT_l, r_xnT_l = (xnT_alt, r_xnT_alt) if use_alt else (xnT, r_xnT)

            if kind == "prior":
                if use_alt:
                    alias(XALT, [r_uT, r_tA, r_tB] + r_diff + r_kvs)
            else:
                alias(XA, XD + XC + XALT)
            for tb in range(NB):
                xsb, rxs = xs_b[tb % 2], r_xs[tb % 2]
                OP("sp", lambda e, tb=tb, xsb=xsb: e.dma_start(out=xsb[:pn, :], in_=x_rows[tb * 128:tb * 128 + pn, :]),
                   reads=[], writes=[rxs], dma=True)
                rmsnorm_rows(xsb[:pn, :], rxs, pn, xnt_l[:pn, tb, :], r_xnt_l[tb], tb)
            ck("A: norm done")
            transposes_to_T(xnt_l, r_xnt_l, NB, pn, N, gmix, xnT_l, r_xnT_l)
            ck("A: transposes done")

            if full or last_prior:
                for p in range(2):
                    wsl, rw = wget(wsrc(w_in, 0, D, 512 * p, 512 * p + 512), DC)
                    if full:
                        for mcl in range(4):
                            mc = 4 * p + mcl
                            b = proj_fm(wsl, rw, mcl, DC, lambda kc: xnT_l[:, kc, :N], r_xnT_l, N)
                            OP("act", lambda e, b=b, mc=mc: e.copy(uT[:, mc, 16:16 + N], banks[b][:, :N]),
                               reads=[r_bank[b]], writes=[r_uT])
                            rel(b)
                    else:
                        for mcl in range(4):
                            mc = 4 * p + mcl
                            b = proj_fm(wsl, rw, mcl, DC, lambda kc: xnT_l[:, kc, N - 16:N], r_xnT_l, 16)
                            OP("act", lambda e, b=b, mc=mc: e.copy(uhist[:, mc, :], banks[b][:, :16]),
                               reads=[r_bank[b]], writes=[r_uhist])
                            rel(b)
                    if need_utok:
                        b = acq()
                        for kc in range(DC):
                            OP("pe", lambda e, kc=kc, b=b, wsl=wsl: e.matmul(
                                banks[b][:NS, :512], xnT_l[:, kc, N - 16:N], wsl[:, kc, :512],
                                start=(kc == 0), stop=(kc == DC - 1)),
                               reads=[rw, r_xnT_l[kc]], writes=[r_bank[b]])
                        OP("act", lambda e, b=b, p=p: e.copy(utok[:NS, p * 512:(p + 1) * 512], banks[b][:NS, :512]),
                           reads=[r_bank[b]], writes=[r_utok])
                        rel(b)
                    wrel()
                if need_utok:
                    dst = pool_p if kind == "own" else pool_s
                    OP("sp", lambda e, dst=dst: e.dma_start(out=dst[:, :], in_=utok[1:NS, :]),
                       reads=[r_utok], writes=[r_out], dma=True)
                    last_out.append(1)

            ck("A: u proj done")
            if full:
                for p in range(2):
                    wsl, rw = wget(wsrc(w_in, 0, D, PW + 512 * p, PW + 512 * p + 512), DC)
                    for mcl in range(4):
                        hh = 4 * p + mcl
                        b = proj_fm(wsl, rw, mcl, DC, lambda kc: xnT_l[:, kc, :N], r_xnT_l, N)
                        OP("act", lambda e, b=b, hh=hh: e.activation(
                            out=QT[:, hh, :N], in_=banks[b][:, :N], func=AF.Copy, scale=float(HD ** -0.5)),
                           reads=[r_bank[b]], writes=[r_QT[hh]])
                        rel(b)
                    wrel()

            ck("A: q proj done")
            nko, nvo = (nk_own, nv_own) if kind == "own" else (nk_s, nv_s)
            ksi = 0
            for which in range(2):
                dstb, r_dstb = (kb, r_kb) if which == 0 else (vb, r_vb)
                oten = nko if which == 0 else nvo
                for p in range(2):
                    c0 = 2 * PW + which * PW + 512 * p
                    wsl, rw = wget(wsrc(w_in, 0, D, c0, c0 + 512), DC)
                    for tb in range(NB):
                        b = acq()
                        for kc in range(DC):
                            OP("pe", lambda e, kc=kc, b=b, tb=tb, wsl=wsl: e.matmul(
                                banks[b][:pn, :512], xnT_l[:, kc, tb * 128:tb * 128 + pn], wsl[:, kc, :512],
                                start=(kc == 0), stop=(kc == DC - 1)),
                               reads=[rw, r_xnT_l[kc]], writes=[r_bank[b]])
                        if full:
                            ki = ksi % 4
                            ksi += 1
                            OP("act", lambda e, b=b, ki=ki: e.copy(kvs[ki][:pn, :], banks[b][:pn, :512]),
                               reads=[r_bank[b]], writes=[r_kvs[ki]])
                            r0 = (g * TN if kind == "own" else 0) + tb * 128
                            OP("sp", lambda e, ki=ki, oten=oten, r0=r0, p=p: e.dma_start(
                                out=oten[r0:r0 + pn, 512 * p:512 * p + 512], in_=kvs[ki][:pn, :]),
                               reads=[r_kvs[ki]], writes=[r_out], dma=True)
                        if which == 0:
                            OP("dve", lambda e, b=b, tb=tb, p=p: e.tensor_copy(
                                kb[:pn, tb, 512 * p:512 * p + 512], banks[b][:pn, :512]),
                               reads=[r_bank[b]], writes=[r_dstb])
                        else:
                            OP("dve", lambda e, b=b, tb=tb, p=p: e.tensor_copy(
                                vb[:pn, 4 * p:4 * p + 4, tb, :], banks[b][:pn, :512].rearrange("p (h d) -> p h d", h=4)),
                               reads=[r_bank[b]], writes=[r_dstb])
                        rel(b)
                    wrel()
            ck("A: kv proj done")
            for h2 in range(NH // 2):
                b = acq()
                bb = banks[b][:, :].bitcast(BF16).rearrange("p (c n) -> p c n", c=2)
                for kk in range(2):
                    hh = 2 * h2 + kk
                    for tb in range(NB):
                        OP("pe", lambda e, kk=kk, hh=hh, tb=tb, bb=bb: e.transpose(
                            out=bb[:, kk, tb * 128:tb * 128 + pn], in_=kb[:pn, tb, hh * 128:(hh + 1) * 128],
                            identity=ident[:pn, :pn]),
                           reads=[r_kb, r_const], writes=[r_bank[b]])
                OP("dve", lambda e, h2=h2, bb=bb: e.tensor_copy(KTt[:, 2 * h2:2 * h2 + 2, :N], bb[:, :, :N]),
                   reads=[r_bank[b]], writes=[r_KTt])
                rel(b)
            ck("A: KT transposes done")
            if kind == "sample":
                for hh in range(NH):
                    OP("sp", lambda e, hh=hh: e.dma_start(out=kts_scr[:, hh, :], in_=KTt[:, hh, :NS]),
                       reads=[r_KTt], writes=[r_sscr], dma=True)
                for hh in range(NH):
                    OP("sp", lambda e, hh=hh: e.dma_start(out=vs_scr[:, hh * HD:(hh + 1) * HD], in_=vb[:NS, hh, 0, :]),
                       reads=[r_vb], writes=[r_sscr], dma=True)
            else:
                ti = tok0 // TN
                for hh in range(NH):
                    OP("sp", lambda e, hh=hh: e.dma_start(out=kt_scr[:, hh, tok0:tok0 + N], in_=KTt[:, hh, :N]),
                       reads=[r_KTt], writes=[r_ktscr[ti]], dma=True)
                for hh in range(NH):
                    OP("sp", lambda e, hh=hh: e.dma_start(
                        out=v_scr.rearrange("p h j d -> p h (j d)")[:, hh, tok0:tok0 + N],
                        in_=vb[:, hh, :, :].rearrange("p t d -> p (t d)")),
                       reads=[r_vb], writes=[r_vscr[ti]], dma=True)
            ck("prior tile done")
            if not full:
                return

            if kind == "sample":
                OP("sp", lambda e: e.dma_start(out=xs_b[0][:15, :PW], in_=sp_in[:, :]),
                   reads=[], writes=[r_xs[0]], dma=True)
                b = acq()
                bv = banks[b][:, :].rearrange("p (c n) -> p c n", c=8)
                for c in range(8):
                    OP("pe", lambda e, c=c, bv=bv: e.transpose(
                        out=bv[:, c, 0:16], in_=xs_b[0][:16, c * 128:(c + 1) * 128], identity=identf[:16, :16]),
                       reads=[r_xs[0], r_const], writes=[r_bank[b]])
                OP("dve", lambda e, bv=bv: e.tensor_copy(uT[:, :, 1:16], bv[:, :, 0:15]),
                   reads=[r_bank[b]], writes=[r_uT])
                rel(b)

            ck("A done")
            L0 = 16 + N
            if kind == "own":
                OP("dve", lambda e: e.tensor_copy(uT[:, :, 0:16], uhist[:, :, :]), reads=[r_uhist], writes=[r_uT])
            for gi, w in enumerate(WIN):
                c0 = 2 * gi
                cur, rcur = uT[:, c0:c0 + 2, :], r_uT
                lo = 0
                bufs = [(tA, r_tA), (tB, r_tB)]
                for k in range(gi + 1):
                    sh = 1 << k
                    lo += sh
                    dstt, rd = bufs[k % 2]
                    OP("dve", lambda e, cur=cur, dstt=dstt, lo=lo, sh=sh: e.tensor_tensor(
                        dstt[:, :, lo:L0], cur[:, :, lo:L0], cur[:, :, lo - sh:L0 - sh], ALU.add),
                       reads=[rcur], writes=[rd])
                    cur, rcur = dstt, rd
                OP("dve", lambda e, cur=cur, c0=c0, w=w: e.scalar_tensor_tensor(
                    diffT[:, c0:c0 + 2, :N], cur[:, :, 16:16 + N], 1.0 / w, uT[:, c0:c0 + 2, 16:16 + N],
                    ALU.mult, ALU.subtract),
                   reads=[rcur, r_uT], writes=[r_diff[gi]])
                if kind == "own" and g == 0:
                    for cc in range(2):
                        OP("dve", lambda e, cur=cur, cc=cc, gi=gi: e.tensor_tensor(
                            cur[:, cc, 16:32], cur[:, cc, 16:32], cntinv[:, gi, :], ALU.mult),
                           reads=[rcur, r_const], writes=[rcur])
                        OP("dve", lambda e, cur=cur, cc=cc, c0=c0: e.tensor_tensor(
                            diffT[:, c0 + cc, 0:16], cur[:, cc, 16:32], uT[:, c0 + cc, 16:32], ALU.subtract),
                           reads=[rcur, r_uT], writes=[r_diff[gi]])
                for ecl in range(2):
                    b = acq()
                    for cc in range(2):
                        OP("pe", lambda e, b=b, gi=gi, cc=cc, ecl=ecl, c0=c0: e.matmul(
                            banks[b][:, :N], wpool[:, gi, cc, ecl * 128:(ecl + 1) * 128], diffT[:, c0 + cc, :N],
                            start=(cc == 0), stop=(cc == 1)),
                           reads=[r_const, r_diff[gi]], writes=[r_bank[b]])
                    OP("dve", lambda e, b=b, c0=c0, ecl=ecl: e.tensor_scalar(
                        oaT[:, c0 + ecl, :N], banks[b][:, :N], spool[:, c0 + ecl:c0 + ecl + 1], None, ALU.mult),
                       reads=[r_bank[b], r_const], writes=[r_oaT[c0 + ecl]])
                    rel(b)
            if kind == "own" and not last_own:
                OP("dve", lambda e: e.tensor_copy(uhist[:, :, :], uT[:, :, N:N + 16]), reads=[r_uT], writes=[r_uhist])

            ck("B done")
            alias(XC, XA)
            if kind == "own":
                nkb = (tok0 + N) // 128
                kblocks = [(j, 128, (j - (nkb - NB)) if j >= nkb - NB else None) for j in range(nkb)]
            else:
                nkb = past_blks + 1
                kblocks = [(j, 128, None) for j in range(past_blks)] + [(past_blks, NS, 0)]

            def load_ctx(hh):
                hb = hh % 2
                if kind == "own":
                    ntile = (tok0 + N) // TN
                    OP("sp", lambda e: e.dma_start(out=KTh[hb][:, :nkb * 128], in_=kt_scr[:, hh, 0:nkb * 128]),
                       reads=r_ktscr[:ntile], writes=[r_KTh[hb]], dma=True)
                    OP("sp", lambda e: e.dma_start(out=Vh[hb][:, :nkb, :], in_=v_scr[:, hh, 0:nkb, :]),
                       reads=r_vscr[:ntile], writes=[r_Vh[hb]], dma=True)
                else:
                    OP("pool", lambda e: e.dma_start(
                        out=ckst[:, :past_blks, :], in_=ck_in[:, hh * HD:(hh + 1) * HD].rearrange("(j p) d -> p j d", p=128)),
                       reads=[], writes=[r_ckst], dma=True)
                    OP("pool", lambda e: e.dma_start(
                        out=Vh[hb][:, :past_blks, :], in_=cv_in[:, hh * HD:(hh + 1) * HD].rearrange("(j p) d -> p j d", p=128)),
                       reads=[], writes=[r_Vh[hb]], dma=True)
                    OP("sp", lambda e: e.dma_start(out=Vh[hb][:NS, past_blks, :], in_=vs_scr[:, hh * HD:(hh + 1) * HD]),
                       reads=[r_sscr], writes=[r_Vh[hb]], dma=True)
                    OP("sp", lambda e: e.dma_start(out=KTh[hb][:, PAST:PAST + NS], in_=kts_scr[:, hh, :]),
                       reads=[r_sscr], writes=[r_KTh[hb]], dma=True)
                    j0 = 0
                    while j0 < past_blks:
                        nj = min(8, past_blks - j0)
                        b = acq()
                        bb = banks[b][:, :].bitcast(BF16)
                        for jj in range(nj):
                            OP("pe", lambda e, jj=jj, j0=j0, bb=bb: e.transpose(
                                out=bb[:, jj * 128:(jj + 1) * 128], in_=ckst[:, j0 + jj, :], identity=ident[:, :]),
                               reads=[r_ckst, r_const], writes=[r_bank[b]])
                        OP("dve", lambda e, j0=j0, nj=nj, bb=bb: e.tensor_copy(
                            KTh[hb][:, j0 * 128:(j0 + nj) * 128], bb[:, :nj * 128]),
                           reads=[r_bank[b]], writes=[r_KTh[hb]])
                        rel(b)
                        j0 += nj

            NBF = 3
            units = []
            for hh in range(NH):
                rb = list(reversed(kblocks))
                for idx, (j, kp, jj) in enumerate(rb):
                    units.append((hh, j, kp, jj, idx == 0, idx == len(rb) - 1))
            ubank = {}
            obank = {}

            zb1, zb2 = {}, {}

            def uinfo(n):
                hh, j, kp, jj, first, last = units[n]
                return hh, j, kp, jj, first, last, hh % 2, n % NBF

            def P1(n):
                hh, j, kp, jj, first, last, hb, i3 = uinfo(n)
                c0 = 128 * jj if jj is not None else 0
                kslice = KTh[hb][:, j * 128:j * 128 + kp]
                z1 = zb1[n] = acq()
                OP("pe", lambda e: e.matmul(banks[z1][:kp, c0:N], kslice, QT[:, hh, c0:N], start=True, stop=True),
                   reads=[r_KTh[hb], r_QT[hh]], writes=[r_bank[z1]])

            def A1(n):
                hh, j, kp, jj, first, last, hb, i3 = uinfo(n)
                c0 = 128 * jj if jj is not None else 0
                z1 = zb1.pop(n)
                OP("act", lambda e: e.activation(out=Eb[i3][:kp, c0:N], in_=banks[z1][:kp, c0:N], func=AF.Exp),
                   reads=[r_bank[z1]], writes=[r_E[i3]])
                rel(z1)

            def A2(n):
                hh, j, kp, jj, first, last, hb, i3 = uinfo(n)
                c0 = 128 * jj if jj is not None else 0
                OP("act", lambda e: e.activation(out=Lb[i3][:kp, c0:N], in_=Eb[i3][:kp, c0:N], func=AF.Ln, bias=1.0),
                   reads=[r_E[i3]], writes=[r_L[i3]])
                if jj is not None:
                    OP("dve", lambda e: e.tensor_tensor(Lb[i3][:kp, c0:N], Lb[i3][:kp, c0:N], masks[:kp, jj, c0:N], ALU.mult),
                       reads=[r_L[i3], r_const], writes=[r_L[i3]])

            def P2(n):
                hh, j, kp, jj, first, last, hb, i3 = uinfo(n)
                c0 = 128 * jj if jj is not None else 0
                ip = (n - 1) % NBF
                kslice = KTh[hb][:, j * 128:j * 128 + kp]
                z2 = zb2[n] = acq()
                OP("pe", lambda e: e.matmul(banks[z2][:kp, c0:N], kslice, QT[:, hh, c0:N], start=True, stop=False),
                   reads=[r_KTh[hb], r_QT[hh]], writes=[r_bank[z2]])
                OP("pe", lambda e: e.matmul(banks[z2][:kp, c0:N], negtri[:kp, :kp], Lb[i3][:kp, c0:N], start=False, stop=first),
                   reads=[r_const, r_L[i3]], writes=[r_bank[z2]])
                if not first:
                    OP("pe", lambda e: e.matmul(banks[z2][:kp, c0:N], negones[:, :kp], Sbb[ip][:, c0:N], start=False, stop=True),
                       reads=[r_const, r_Sb[ip]], writes=[r_bank[z2]])

            def A3(n):
                hh, j, kp, jj, first, last, hb, i3 = uinfo(n)
                c0 = 128 * jj if jj is not None else 0
                z2 = zb2.pop(n)
                OP("act", lambda e: e.activation(out=Ab[i3][:kp, c0:N], in_=banks[z2][:kp, c0:N], func=AF.Exp),
                   reads=[r_bank[z2]], writes=[r_A[i3]])
                rel(z2)
                if jj is not None:
                    OP("dve", lambda e: e.tensor_tensor(Ab[i3][:kp, c0:N], Ab[i3][:kp, c0:N], masks[:kp, jj, c0:N], ALU.mult),
                       reads=[r_A[i3], r_const], writes=[r_A[i3]])

            def D1(n):
                hh, j, kp, jj, first, last, hb, i3 = uinfo(n)
                c0 = 128 * jj if jj is not None else 0
                if first:
                    OP("dve", lambda e: e.memset(S32[:, :N], 0.0), reads=[], writes=[r_S32])
                if not last:
                    OP("dve", lambda e: e.tensor_tensor(S32[:kp, c0:N], S32[:kp, c0:N], Lb[i3][:kp, c0:N], ALU.add),
                       reads=[r_S32, r_L[i3]], writes=[r_S32])
                    OP("dve", lambda e: e.tensor_copy(Sbb[i3][:, :N], S32[:, :N]),
                       reads=[r_S32], writes=[r_Sb[i3]])

            def P3(n):
                hh, j, kp, jj, first, last, hb, i3 = uinfo(n)
                c0 = 128 * jj if jj is not None else 0
                if first:
                    obank[hh] = acq()
                bo = obank[hh]
                OP("pe", lambda e: e.matmul(banks[bo][:, c0:N], Vh[hb][:kp, j, :], Ab[i3][:kp, c0:N], start=first, stop=last,
                                                 skip_group_check=True),
                   reads=[r_Vh[hb], r_A[i3]], writes=[r_bank[bo]])
                if last:
                    OP("dve", lambda e: e.tensor_copy(obT[:, hh, :N], banks[bo][:, :N]),
                       reads=[r_bank[bo]], writes=[r_obT[hh]])
                    rel(bo)
                    if hh + 2 < NH:
                        load_ctx(hh + 2)

            load_ctx(0)
            load_ctx(1)
            nu = len(units)
            assert nu // NH >= 4

            def run(fn, n):
                if 0 <= n < nu:
                    fn(n)

            for t in range(nu + 4):
                run(P1, t)
                run(A1, t - 1)
                run(P2, t - 2)
                run(A3, t - 3)
                run(A2, t - 1)
                run(D1, t - 2)
                run(P3, t - 4)

            ck("C done")
            alias(XD, XC + XA)
            for tb in range(NB):
                OP("sp", lambda e, tb=tb: e.dma_start(out=ht[:pn, tb, :], in_=x_rows[tb * 128:tb * 128 + pn, :]),
                   reads=[], writes=[r_ht[tb]], dma=True)
            for p in range(4):
                wb, rwb = wget(wsrc(w_branch, 0, D, 512 * p, 512 * p + 512), DC)
                wga, rwga = wget(wsrc(w_in, 0, D, 4 * PW + 512 * p, 4 * PW + 512 * p + 512), DC)
                bybs = []
                for mcl in range(4):
                    bya = proj_fm(wb, rwb, mcl, 8, lambda kc: oaT[:, kc, :N], r_oaT, N)
                    bga = proj_fm(wga, rwga, mcl, DC, lambda kc: xnT[:, kc, :N], r_xnT, N)
                    byb = proj_fm(wb, rwb, mcl, 8, lambda kc: obT[:, kc, :N], r_obT, N, kofs=8)
                    bybs.append(byb)
                    OP("act", lambda e, bga=bga: e.activation(out=sAb[0][:, :N], in_=banks[bga][:, :N], func=AF.Sigmoid),
                       reads=[r_bank[bga]], writes=[r_sA[0]])
                    OP("dve", lambda e, bya=bya, mcl=mcl: e.tensor_tensor(t1b[mcl][:, :N], banks[bya][:, :N], sAb[0][:, :N], ALU.mult),
                       reads=[r_bank[bya], r_sA[0]], writes=[r_t1[mcl]])
                    rel(bya)
                    rel(bga)
                wrel()
                wrel()
                wgb, rwgb = wget(wsrc(w_in, 0, D, 4 * PW + D + 512 * p, 4 * PW + D + 512 * p + 512), DC)
                for mcl in range(4):
                    mc = 4 * p + mcl
                    byb = bybs[mcl]
                    bgb = proj_fm(wgb, rwgb, mcl, DC, lambda kc: xnT[:, kc, :N], r_xnT, N)
                    OP("act", lambda e, bgb=bgb: e.activation(out=sBb[0][:, :N], in_=banks[bgb][:, :N], func=AF.Sigmoid),
                       reads=[r_bank[bgb]], writes=[r_sB[0]])
                    OP("dve", lambda e, byb=byb: e.tensor_tensor(sBb[0][:, :N], banks[byb][:, :N], sBb[0][:, :N], ALU.mult),
                       reads=[r_bank[byb], r_sB[0]], writes=[r_sB[0]])
                    OP("dve", lambda e, mc=mc, mcl=mcl: e.tensor_tensor(mergedT[:, mc, :N], t1b[mcl][:, :N], sBb[0][:, :N], ALU.add),
                       reads=[r_t1[mcl], r_sB[0]], writes=[r_mer[mc]])
                    rel(bgb)
                    rel(byb)
                wrel()

            ck("D done")
            for cg in range(4):
                wsl, rw = wget(wsrc(w_out, 0, D, 512 * cg, 512 * cg + 512), DC)
                for tb in range(NB):
                    b = acq()
                    for kc in range(DC):
                        OP("pe", lambda e, kc=kc, b=b, tb=tb, wsl=wsl: e.matmul(
                            banks[b][:pn, :512], mergedT[:, kc, tb * 128:tb * 128 + pn], wsl[:, kc, :512],
                            start=(kc == 0), stop=(kc == DC - 1)),
                           reads=[rw, r_mer[kc]], writes=[r_bank[b]])
                    OP("dve", lambda e, b=b, tb=tb, cg=cg: e.tensor_tensor(
                        ht[:pn, tb, 512 * cg:512 * cg + 512], banks[b][:pn, :512], ht[:pn, tb, 512 * cg:512 * cg + 512], ALU.add),
                       reads=[r_bank[b], r_ht[tb]], writes=[r_ht[tb]])
                    rel(b)
                wrel()

            ck("E done")
            alias(r_n2t, r_mer)
            for tb in range(NB):
                rmsnorm_rows(ht[:pn, tb, :], r_ht[tb], pn, n2t[:pn, tb, :], r_n2t[tb], tb)
            transposes_to_T(n2t, r_n2t, NB, pn, N, gffn)

            ck("F done")
            alias(r_hid, r_n2t + r_mer)
            gi_ = 0
            for pp in range(FC // 4):
                wg, rwg = wget(wsrc(w_gu, 0, D, 512 * pp, 512 * pp + 512), DC)
                wu, rwu = wget(wsrc(w_gu, 0, D, DFF + 512 * pp, DFF + 512 * pp + 512), DC)
                for fcl in range(4):
                    fc = 4 * pp + fcl
                    i2 = gi_ % 2
                    gi_ += 1
                    bg = proj_fm(wg, rwg, fcl, DC, lambda kc: xnT[:, kc, :N], r_xnT, N)
                    bu = proj_fm(wu, rwu, fcl, DC, lambda kc: xnT[:, kc, :N], r_xnT, N)
                    OP("act", lambda e, bg=bg, i2=i2: e.activation(out=t1b[i2][:, :N], in_=banks[bg][:, :N], func=AF.Silu),
                       reads=[r_bank[bg]], writes=[r_t1[i2]])
                    OP("dve", lambda e, bu=bu, i2=i2, fc=fc: e.tensor_tensor(hidT[:, fc, :N], banks[bu][:, :N], t1b[i2][:, :N], ALU.mult),
                       reads=[r_bank[bu], r_t1[i2]], writes=[r_hid[fc]])
                    rel(bg)
                    rel(bu)
                wrel()
                wrel()

            ck("G done")
            kparts = [(0, 16), (16, 32), (32, FC)]
            for cg in range(4):
                bs = [acq() for _ in range(NB)]
                for (k0, k1) in kparts:
                    wsl, rw = wget(wsrc(w_down, k0 * 128, k1 * 128, 512 * cg, 512 * cg + 512), k1 - k0)
                    for tb in range(NB):
                        for kc in range(k0, k1):
                            OP("pe", lambda e, kc=kc, k0=k0, tb=tb, wsl=wsl, b=bs[tb]: e.matmul(
                                banks[b][:pn, :512], hidT[:, kc, tb * 128:tb * 128 + pn], wsl[:, kc - k0, :512],
                                start=(kc == 0), stop=(kc == FC - 1)),
                               reads=[rw, r_hid[kc]], writes=[r_bank[bs[tb]]])
                    wrel()
                for tb in range(NB):
                    OP("dve", lambda e, b=bs[tb], tb=tb, cg=cg: e.tensor_tensor(
                        ht[:pn, tb, 512 * cg:512 * cg + 512], banks[b][:pn, :512], ht[:pn, tb, 512 * cg:512 * cg + 512], ALU.add),
                       reads=[r_bank[bs[tb]], r_ht[tb]], writes=[r_ht[tb]])
                    rel(bs[tb])

            ck("H done")
            alias(r_n2t, r_hid)
            yout = y_own if kind == "own" else y_s
            for tb in range(NB):
                rmsnorm_rows(ht[:pn, tb, :], r_ht[tb], pn, ht[:pn, tb, :], r_ht[tb], tb, gscale=gfin[:pn, :])
                r0 = (g * TN if kind == "own" else 0) + tb * 128
                OP("sp", lambda e, tb=tb, r0=r0: e.dma_start(out=yout[r0:r0 + pn, :], in_=ht[:pn, tb, :]),
                   reads=[r_ht[tb]], writes=[r_out], dma=True)

        try:
            ck("consts")
            for g in range(n_prior_tiles):
                tile("prior", g, xp_pri[g * TN:(g + 1) * TN, :], TN, g * TN)
            for g in range(n_own_tiles):
                tile("own", g, xp_own[g * TN:(g + 1) * TN, :], TN, NPRI + g * TN)
                ck("own tile done")
            tile("sample", 0, xs_in, NS, 0)
        except _Stop:
            pass
        return wreq

    for r in R.values():
        r.last_w, r.readers = None, []
    wpid, wconv, wscr, r_wscr = {}, {}, None, []
    wlist = flow(Prog(dry=True), None)
    for key, src, kcn in wlist:
        if key not in wpid:
            wpid[key] = len(wpid)
            wconv[key] = (src, kcn)
    wscr = nc.dram_tensor("wscr", [len(wpid), 128, 16, 512], BF16, kind="Internal").ap()
    r_wscr = [Res(f"wscr{i}") for i in range(len(wpid))]
    prog = Prog()
    flow(prog, wlist)

    for e in ("pe", "act", "dve", "pool"):
        n = 0
        for o in prog.streams[e]:
            if o.dma:
                continue
            if o.signaled:
                o.sig = (sems_eng[e][(n // EPOCH) % 4], n % EPOCH + 1)
                n += 1
        assert n < 4 * EPOCH, (e, n)
    nd = 0
    dma_prev = {}
    for e in ("pool", "sp"):
        pass
    allops = []
    for e in Prog.ENGS:
        for o in prog.streams[e]:
            if o.dma:
                allops.append(o)
    qsems = {"pool": sems_dma[:NDL // 2], "sp": sems_dma[NDL // 2:]}
    for e in ("pool", "sp"):
        k = 0
        ss = qsems[e]
        for o in prog.streams[e]:
            if not o.dma:
                continue
            s = ss[k % len(ss)]
            val = 16 * (k // len(ss) + 1)
            o.sig = (s, val)
            o.idx = (s, val - 16)
            k += 1

    def emit_stream(ename, eng):
        waited = {}
        for o in prog.streams[ename]:
            need = {}
            if o.dma and o.idx[1] > 0:
                need[id(o.idx[0])] = (o.idx[0], o.idx[1])
            for d in o.deps:
                s, v = d.sig
                k = id(s)
                if k not in need or need[k][1] < v:
                    need[k] = (s, v)
            for k, (s, v) in need.items():
                if waited.get(k, 0) >= v:
                    continue
                eng.wait_ge(s, v)
                waited[k] = v
            ins = o.fn(eng)
            if o.dma:
                ins.then_inc(o.sig[0], 16)
            elif o.signaled:
                ins.then_inc(o.sig[0], 1)
        if ename in ("sp", "pool"):
            last = {}
            for o in prog.streams[ename]:
                if o.dma:
                    last[id(o.sig[0])] = o.sig
            for s, v in last.values():
                eng.wait_ge(s, v)

    with nc.Block() as block:
        @block.tensor
        def _(e):
            emit_stream("pe", e)

        @block.scalar
        def _(e):
            emit_stream("act", e)

        @block.vector
        def _(e):
            emit_stream("dve", e)

        @block.gpsimd
        def _(e):
            emit_stream("pool", e)

        @block.sync
        def _(e):
            emit_stream("sp", e)
    es.close()
    return nc


def _consts(h_is_first):
    ident = np.eye(128, dtype=np.float32)
    s_ = np.arange(128)
    negtri = -(s_[:, None] >= s_[None, :]).astype(np.float32)
    negones = -np.ones((128, 128), np.float32)
    t_ = np.arange(TN)
    masks = np.stack([((128 * jj + s_)[:, None] < t_[None, :]).astype(np.float32) for jj in range(4)], axis=1)
    cnt = np.empty((4, 16), np.float32)
    for gi, w in enumerate(WIN):
        if h_is_first:
            cnt[gi] = 1.0 / np.minimum(np.arange(16) + 1, w)
        else:
            cnt[gi] = 1.0 / w
    cnt = np.ascontiguousarray(np.broadcast_to(cnt.reshape(1, 64), (128, 64)))
    return ident, negtri, negones, np.ascontiguousarray(masks.reshape(128, 4 * TN)), cnt


def _run(inputs, n_own_tiles, n_prior_tiles, past_blks, batch, dec_batch):
    f = lambda a: np.ascontiguousarray(np.asarray(a, dtype=np.float32))
    x_prompt, x_sample = f(inputs["x_prompt"]), f(inputs["x_sample"])
    cache_k, cache_v, state_pool = f(inputs["cache_k"]), f(inputs["cache_v"]), f(inputs["state_pool"])
    NOWN, NPRI = n_own_tiles * TN, n_prior_tiles * TN
    seq = x_prompt.shape[1]
    assert seq == NOWN + NPRI and NOWN == NPRI
    past = past_blks * 128
    nc = build(n_own_tiles, n_prior_tiles, past_blks)

    def vec16(v):
        return np.ascontiguousarray(f(v).reshape(-1, 128).T)

    shared = {
        "w_in": f(inputs["w_in"][0]), "w_pool": f(inputs["w_pool"][0]), "w_branch": f(inputs["w_branch"][0]),
        "w_out": f(inputs["w_out"][0]), "w_gu": f(inputs["w_gate_up"][0]), "w_down": f(inputs["w_down"][0]),
        "gmix": vec16(inputs["g_mix"][0]), "gffn": vec16(inputs["g_ffn"][0]), "spool": vec16(inputs["s_pool"][0]),
        "gfin": np.ascontiguousarray(np.broadcast_to(f(inputs["g_final"]).reshape(1, D), (128, D))),
    }
    in_maps = []
    ncores = 2 * batch
    assert ncores == dec_batch
    for c in range(ncores):
        b, h = c // 2, c % 2
        ident, negtri, negones, masks, cnt = _consts(h == 0)
        m = dict(shared)
        m["xp_own"] = np.ascontiguousarray(x_prompt[b, h * NOWN:(h + 1) * NOWN])
        m["xp_pri"] = np.ascontiguousarray(x_prompt[b, :NPRI]) if h == 1 else np.zeros((NPRI, D), np.float32)
        m["xs_in"] = np.ascontiguousarray(x_sample[c])
        m["ck_in"] = np.ascontiguousarray(cache_k[0, c].reshape(past, NH * HD))
        m["cv_in"] = np.ascontiguousarray(cache_v[0, c].reshape(past, NH * HD))
        m["sp_in"] = np.ascontiguousarray(state_pool[0, c])
        m.update(ident=ident, negtri=negtri, negones=negones, masks=masks, cntinv=cnt)
        in_maps.append(m)
    res = run_bass_kernel_spmd(nc, in_maps, core_ids=list(range(ncores)))
    rs = res.results
    y_prompt = np.empty((batch, seq, D), np.float32)
    nkp = np.empty((1, batch, seq, NH, HD), np.float32)
    nvp = np.empty((1, batch, seq, NH, HD), np.float32)
    npp = np.empty((1, batch, 15, PW), np.float32)
    y_sample = np.empty((dec_batch, NS, D), np.float32)
    nks = np.empty((1, dec_batch, NS, NH, HD), np.float32)
    nvs = np.empty((1, dec_batch, NS, NH, HD), np.float32)
    nps = np.empty((1, dec_batch, 15, PW), np.float32)
    for c in range(ncores):
        b, h = c // 2, c % 2
        r = rs[c]
        y_prompt[b, h * NOWN:(h + 1) * NOWN] = r["y_own"]
        nkp[0, b, h * NOWN:(h + 1) * NOWN] = r["nk_own"].reshape(NOWN, NH, HD)
        nvp[0, b, h * NOWN:(h + 1) * NOWN] = r["nv_own"].reshape(NOWN, NH, HD)
        if h == 1:
            npp[0, b] = r["pool_p"]
        y_sample[c] = r["y_s"]
        nks[0, c] = r["nk_s"].reshape(NS, NH, HD)
        nvs[0, c] = r["nv_s"].reshape(NS, NH, HD)
        nps[0, c] = r["pool_s"]
    return (y_prompt, y_sample, nkp, nvp, npp, nks, nvs, nps)


def kernel(**inputs):
    return _run(inputs, 4, 4, 32, 4, 8)
```
